# Optimizing a Trainium2 kernel written in Bass

```python
import jax
import jax.numpy as jnp
from jax import lax
import numpy as np


D_MODEL = 2048
BATCH = 1
SEQ = 16384
DEPTH = 2

HEAD_DIM = 128
ROT_DIM = HEAD_DIM // 4
ROPE_THETA = 500000.0
N_BRANCH = 4
NORM_EPS = 1e-6

GLA_HEADS = 4
GLA_DK = 64
GLA_DV = 128
GLA_RANK = 16
GLA_NORMALIZER = 16.0
GLA_CHUNK = 64

DIL_PAIRS = ((128, 1), (512, 4), (2048, 16))
DIL_HEADS_PER_GROUP = 2
DIL_HEADS = DIL_HEADS_PER_GROUP * len(DIL_PAIRS)

RWKV_HEAD_SIZE = 64
RWKV_HEADS = 8
RWKV_WIDTH = RWKV_HEADS * RWKV_HEAD_SIZE
RWKV_DECAY_LORA = 96
RWKV_AAA_LORA = 96
RWKV_MV_LORA = 64
RWKV_GATE_LORA = 256
RWKV_LNX_EPS = 64e-5

MOBA_HEADS = 4
MOBA_BLOCK = 256
MOBA_TOPK = 3
MOBA_QCHUNK = 128

FFN_HIDDEN = -(-8 * D_MODEL // (3 * 256)) * 256

GLA_SIZES = (GLA_HEADS * GLA_DK, GLA_HEADS * GLA_DK, GLA_HEADS * GLA_DV, GLA_HEADS * GLA_DV, GLA_RANK)
RWKV_SIZES = (RWKV_WIDTH, RWKV_WIDTH, RWKV_WIDTH, RWKV_DECAY_LORA, RWKV_AAA_LORA, RWKV_GATE_LORA)
GLA_IN = sum(GLA_SIZES)
DIL_IN = 3 * DIL_HEADS * HEAD_DIM
RWKV_IN = sum(RWKV_SIZES)
MOBA_IN = 3 * MOBA_HEADS * HEAD_DIM
IN_SIZES = (N_BRANCH * D_MODEL, GLA_IN, DIL_IN, RWKV_IN, MOBA_IN)
IN_TOTAL = sum(IN_SIZES)
GLA_OUT = GLA_HEADS * GLA_DV
DIL_OUT = DIL_HEADS_PER_GROUP * HEAD_DIM
RWKV_OUT = RWKV_WIDTH
MOBA_OUT = MOBA_HEADS * HEAD_DIM

kernel_name = 'hybrid_gated_parallel_mixers_adaln'


def split_sizes(t, sizes):
    offsets = [int(o) for o in np.cumsum(sizes)[:-1]]
    return jnp.split(t, offsets, axis=-1)


def rms_norm(x, gain):
    xf = x.astype(jnp.float32)
    y = xf * lax.rsqrt(jnp.mean(xf * xf, axis=-1, keepdims=True) + NORM_EPS)
    return (y * gain.astype(jnp.float32)).astype(x.dtype)


def to_heads(t, n_heads):
    B, S, _ = t.shape
    return t.reshape(B, S, n_heads, -1).transpose(0, 2, 1, 3)


def rope_tables(positions):
    inv_freq = ROPE_THETA ** (-jnp.arange(0, ROT_DIM, 2, dtype=jnp.float32) / ROT_DIM)
    ang = positions.astype(jnp.float32)[..., None] * inv_freq
    return jnp.cos(ang)[:, None], jnp.sin(ang)[:, None]


def apply_rope(x, cos, sin):
    x_rot, x_pass = x[..., :ROT_DIM], x[..., ROT_DIM:]
    x1, x2 = jnp.split(x_rot, 2, axis=-1)
    rotated = jnp.concatenate([x1 * cos - x2 * sin, x2 * cos + x1 * sin], axis=-1)
    return jnp.concatenate([rotated.astype(x.dtype), x_pass], axis=-1)


def gla_mixer(p, w_a2, b_a2, g_norm):
    dtype = p.dtype
    B, S, _ = p.shape
    q, k, v, g, a_low = split_sizes(p.astype(jnp.float32), GLA_SIZES)
    gk = jax.nn.log_sigmoid(a_low @ w_a2.astype(jnp.float32) + b_a2.astype(jnp.float32)) / GLA_NORMALIZER
    n = S // GLA_CHUNK

    def chunked(t):
        return t.reshape(B, n, GLA_CHUNK, GLA_HEADS, -1).transpose(1, 0, 3, 2, 4)

    causal = jnp.tril(jnp.ones((GLA_CHUNK, GLA_CHUNK), dtype=bool))[:, :, None]

    def step(state, inp):
        q_c, k_c, v_c, g_c = inp
        b = jnp.cumsum(g_c, axis=-2)
        b_last = b[:, :, -1:, :]
        rel = jnp.exp(jnp.where(causal, b[:, :, :, None, :] - b[:, :, None, :, :], -jnp.inf))
        scores = jnp.einsum('bhtd,bhsd,bhtsd->bhts', q_c, k_c, rel)
        out = (jnp.einsum('bhtd,bhde->bhte', q_c * jnp.exp(b), state)
               + jnp.einsum('bhts,bhse->bhte', scores, v_c))
        state = (jnp.exp(b_last)[:, :, 0, :, None] * state
                 + jnp.einsum('bhsd,bhse->bhde', k_c * jnp.exp(b_last - b), v_c))
        return state, out

    state0 = jnp.zeros((B, GLA_HEADS, GLA_DK, GLA_DV), jnp.float32)
    _, o = lax.scan(step, state0, (chunked(q * GLA_DK ** -0.5), chunked(k), chunked(v), chunked(gk)))
    o = o.transpose(1, 0, 3, 2, 4).reshape(B, S, GLA_HEADS, GLA_DV)
    o = o * lax.rsqrt(jnp.mean(o * o, axis=-1, keepdims=True) + NORM_EPS) * g_norm.astype(jnp.float32)
    return (o.reshape(B, S, GLA_OUT) * jax.nn.silu(g)).astype(dtype)


def banded_attention(q, k, v, window):
    N, L, dh = q.shape
    blk = window
    nb = -(-L // blk)
    pad = nb * blk - L

    def blocks(t):
        return jnp.pad(t, ((0, 0), (0, pad), (0, 0))).reshape(N, nb, blk, dh)

    def with_prev(t):
        return jnp.concatenate([jnp.pad(t[:, :-1], ((0, 0), (1, 0), (0, 0), (0, 0))), t], axis=2)

    qb = blocks(q)
    k2, v2 = with_prev(blocks(k)), with_prev(blocks(v))
    qi = jnp.arange(blk)[:, None]
    ki = jnp.arange(2 * blk)[None, :]
    dist = blk + qi - ki
    band = (dist >= 0) & (dist <= window)
    mask = band[None] & ((jnp.arange(nb)[:, None, None] > 0) | (ki >= blk)[None])
    s = jnp.einsum('nbqd,nbkd->nbqk', qb, k2).astype(jnp.float32) * dh ** -0.5
    s = jnp.where(mask, s, -jnp.inf)
    m = jnp.max(s, axis=-1, keepdims=True)
    pr = jnp.exp(s - m)
    den = jnp.sum(pr, axis=-1, keepdims=True)
    o = jnp.einsum('nbqk,nbkd->nbqd', pr, v2.astype(jnp.float32)) / den
    lse = (m + jnp.log(den))[..., 0]
    return o.reshape(N, nb * blk, dh)[:, :L], lse.reshape(N, nb * blk)[:, :L]


def fold_dilated(t, dil):
    B, h, S, d = t.shape
    return t.reshape(B, h, S // dil, dil, d).transpose(0, 1, 3, 2, 4).reshape(B * h * dil, S // dil, d)


def dilated_mixer(p, cos, sin):
    dtype = p.dtype
    B, S, _ = p.shape
    q, k, v = [to_heads(t, DIL_HEADS) for t in jnp.split(p.astype(jnp.float32), 3, axis=-1)]
    q, k = apply_rope(q, cos, sin), apply_rope(k, cos, sin)
    hp = DIL_HEADS_PER_GROUP
    outs, lses = [], []
    for g, (window, dil) in enumerate(DIL_PAIRS):
        hs = slice(g * hp, (g + 1) * hp)
        o, lse = banded_attention(fold_dilated(q[:, hs], dil), fold_dilated(k[:, hs], dil),
                                  fold_dilated(v[:, hs], dil), window // dil)
        outs.append(o.reshape(B, hp, dil, S // dil, HEAD_DIM).transpose(0, 1, 3, 2, 4).reshape(B, hp, S, HEAD_DIM))
        lses.append(lse.reshape(B, hp, dil, S // dil).transpose(0, 1, 3, 2).reshape(B, hp, S))
    alpha = jax.nn.softmax(jnp.stack(lses, axis=0), axis=0)
    o = jnp.sum(alpha[..., None] * jnp.stack(outs, axis=0), axis=0)
    return o.transpose(0, 2, 1, 3).reshape(B, S, DIL_OUT).astype(dtype)


def rwkv_heads(t):
    return t.reshape(t.shape[:-1] + (RWKV_HEADS, RWKV_HEAD_SIZE))


def rwkv7_mixer(p, mu, w0, w2, a0, a2, g2, k_k, k_a, r_k, lnx_w, lnx_b, v_first, v_mix):
    dtype = p.dtype
    B, S, _ = p.shape
    p = p.astype(jnp.float32)
    p_prev = jnp.pad(p[:, :-1], ((0, 0), (1, 0), (0, 0)))
    xs = p + (p_prev - p) * mu.astype(jnp.float32)
    r, k, v, w_low, a_low, g_low = split_sizes(xs, RWKV_SIZES)
    log_w = -jax.nn.softplus(-(w0 + jnp.tanh(w_low) @ w2)) - 0.5
    decay = jnp.exp(-jnp.exp(log_w))
    a = jax.nn.sigmoid(a0 + a_low @ a2)
    g = jax.nn.sigmoid(g_low) @ g2
    if v_mix is None:
        v_first = v
    else:
        v0, v1, v2 = v_mix
        v = v + (v_first - v) * jax.nn.sigmoid(v0 + (v @ v1) @ v2)
    kk = rwkv_heads(k * k_k)
    kk = kk / jnp.maximum(jnp.sqrt(jnp.sum(kk * kk, axis=-1, keepdims=True)), 1e-12)
    k = k * (1.0 + (a - 1.0) * k_a)
    rh, kh, vh, wh, ah = rwkv_heads(r), rwkv_heads(k), rwkv_heads(v), rwkv_heads(decay), rwkv_heads(a)

    def step(state, inp):
        r_t, w_t, k_t, v_t, a_t, b_t = inp
        sa = jnp.einsum('bhij,bhj->bhi', state, a_t)
        state = (state * w_t[..., None, :] + sa[..., :, None] * b_t[..., None, :]
                 + v_t[..., :, None] * k_t[..., None, :])
        return state, jnp.einsum('bhij,bhj->bhi', state, r_t)

    def seq_first(t):
        return t.transpose(1, 0, 2, 3)

    state0 = jnp.zeros((B, RWKV_HEADS, RWKV_HEAD_SIZE, RWKV_HEAD_SIZE), jnp.float32)
    _, y = lax.scan(step, state0, (seq_first(rh), seq_first(wh), seq_first(kh), seq_first(vh),
                                   seq_first(-kk), seq_first(kk * ah)))
    y = y.transpose(1, 0, 2, 3)
    mean = jnp.mean(y, axis=-1, keepdims=True)
    var = jnp.mean(jnp.square(y - mean), axis=-1, keepdims=True)
    y = ((y - mean) * lax.rsqrt(var + RWKV_LNX_EPS)).reshape(B, S, RWKV_WIDTH) * lnx_w + lnx_b
    bonus = jnp.sum(rh * kh * rwkv_heads(r_k), axis=-1, keepdims=True) * vh
    y = y + bonus.reshape(B, S, RWKV_WIDTH)
    return (y * g).astype(dtype), v_first


def moba_mixer(p, cos, sin):
    dtype = p.dtype
    B, S, _ = p.shape
    H, blk, qc_len = MOBA_HEADS, MOBA_BLOCK, MOBA_QCHUNK
    q, k, v = [to_heads(t, H) for t in jnp.split(p.astype(jnp.float32), 3, axis=-1)]
    q = apply_rope(q, cos, sin) * HEAD_DIM ** -0.5
    k = apply_rope(k, cos, sin)
    nb = -(-S // blk)
    pad = nb * blk - S
    kb = jnp.pad(k, ((0, 0), (0, 0), (0, pad), (0, 0))).reshape(B, H, nb, blk, HEAD_DIM)
    vb = jnp.pad(v, ((0, 0), (0, 0), (0, pad), (0, 0))).reshape(B, H, nb, blk, HEAD_DIM)
    k_mean = jnp.mean(kb, axis=3)
    top_k = min(MOBA_TOPK, nb)
    nq = S // qc_len
    q_chunks = q.reshape(B, H, nq, qc_len, HEAD_DIM).transpose(2, 0, 1, 3, 4)
    b_idx = jnp.arange(B)[:, None, None, None]
    h_idx = jnp.arange(H)[None, :, None, None]

    def chunk_attend(args):
        q_i, ci = args
        q_pos = ci * qc_len + jnp.arange(qc_len)
        cur = (ci * qc_len) // blk
        gate = jnp.einsum('bhqd,bhnd->bhqn', q_i, k_mean)
        gate = jnp.where(jnp.arange(nb) < cur, gate, -jnp.inf)
        sel_score, sel = lax.top_k(gate, top_k)
        sel_ok = jnp.isfinite(sel_score)
        k_sel = kb[b_idx, h_idx, sel]
        v_sel = vb[b_idx, h_idx, sel]
        s_sel = jnp.einsum('bhqd,bhqkld->bhqkl', q_i, k_sel)
        s_sel = jnp.where(sel_ok[..., None], s_sel, -jnp.inf).reshape(B, H, qc_len, top_k * blk)
        k_own = lax.dynamic_index_in_dim(kb, cur, axis=2, keepdims=False)
        v_own = lax.dynamic_index_in_dim(vb, cur, axis=2, keepdims=False)
        s_own = jnp.einsum('bhqd,bhld->bhql', q_i, k_own)
        own_pos = cur * blk + jnp.arange(blk)
        s_own = jnp.where(own_pos[None, :] <= q_pos[:, None], s_own, -jnp.inf)
        probs = jax.nn.softmax(jnp.concatenate([s_sel, s_own], axis=-1), axis=-1)
        p_sel = probs[..., :top_k * blk].reshape(B, H, qc_len, top_k, blk)
        p_own = probs[..., top_k * blk:]
        return (jnp.einsum('bhqkl,bhqkld->bhqd', p_sel, v_sel)
                + jnp.einsum('bhql,bhld->bhqd', p_own, v_own))

    o = lax.map(chunk_attend, (q_chunks, jnp.arange(nq)))
    return o.transpose(1, 0, 3, 2, 4).reshape(B, S, MOBA_OUT).astype(dtype)


def setup_inputs(seed: int = 0) -> dict:
    key = jax.random.key(seed)
    ks = iter(jax.random.split(key, 40))
    L = DEPTH
    W = RWKV_WIDTH

    def nrm(shape, scale):
        return jax.random.normal(next(ks), shape, jnp.float32) * scale

    return {
        'x': nrm((BATCH, SEQ, D_MODEL), 1.0),
        'c': nrm((BATCH, D_MODEL), 1.0),
        'positions': (jnp.arange(SEQ, dtype=jnp.int32)[None, :]
                      + jax.random.randint(next(ks), (BATCH, 1), 0, 1024, dtype=jnp.int32)),
        'w_ada': nrm((L, D_MODEL, 6 * D_MODEL), 0.5 * D_MODEL ** -0.5),
        'b_ada': nrm((L, 6 * D_MODEL), 0.02),
        'norm1': 1.0 + nrm((L, D_MODEL), 0.02),
        'w_in': nrm((L, D_MODEL, IN_TOTAL), D_MODEL ** -0.5),
        'gla_w_a2': nrm((L, GLA_RANK, GLA_HEADS * GLA_DK), GLA_RANK ** -0.5),
        'gla_b_a2': nrm((L, GLA_HEADS * GLA_DK), 0.1),
        'gla_gnorm': 1.0 + nrm((L, GLA_DV), 0.02),
        'rwkv_mu': jax.random.uniform(next(ks), (L, RWKV_IN), jnp.float32),
        'rwkv_w0': -1.0 + nrm((L, W), 0.5),
        'rwkv_w2': nrm((L, RWKV_DECAY_LORA, W), 0.5 * RWKV_DECAY_LORA ** -0.5),
        'rwkv_a0': nrm((L, W), 0.5),
        'rwkv_a2': nrm((L, RWKV_AAA_LORA, W), RWKV_AAA_LORA ** -0.5),
        'rwkv_g2': nrm((L, RWKV_GATE_LORA, W), RWKV_GATE_LORA ** -0.5),
        'rwkv_k_k': 0.85 + nrm((L, W), 0.05),
        'rwkv_k_a': 1.0 + nrm((L, W), 0.05),
        'rwkv_r_k': nrm((L, W), 0.1),
        'rwkv_lnx_w': 1.0 + nrm((L, W), 0.02),
        'rwkv_lnx_b': nrm((L, W), 0.02),
        'rwkv_v0': nrm((L - 1, W), 0.5),
        'rwkv_v1': nrm((L - 1, W, RWKV_MV_LORA), W ** -0.5),
        'rwkv_v2': nrm((L - 1, RWKV_MV_LORA, W), 0.5 * RWKV_MV_LORA ** -0.5),
        'w_branch_a': nrm((L, GLA_OUT, D_MODEL), GLA_OUT ** -0.5),
        'w_branch_b': nrm((L, DIL_OUT, D_MODEL), DIL_OUT ** -0.5),
        'w_branch_c': nrm((L, RWKV_OUT, D_MODEL), RWKV_OUT ** -0.5),
        'w_branch_d': nrm((L, MOBA_OUT, D_MODEL), MOBA_OUT ** -0.5),
        'w_out': nrm((L, D_MODEL, D_MODEL), D_MODEL ** -0.5),
        'norm2': 1.0 + nrm((L, D_MODEL), 0.02),
        'w_ffn_in': nrm((L, D_MODEL, 2 * FFN_HIDDEN), D_MODEL ** -0.5),
        'w_ffn_out': nrm((L, FFN_HIDDEN, D_MODEL), FFN_HIDDEN ** -0.5),
        'norm_f': 1.0 + nrm((D_MODEL,), 0.02),
    }


def reference(x, c, positions, w_ada, b_ada, norm1, w_in, gla_w_a2, gla_b_a2, gla_gnorm,
              rwkv_mu, rwkv_w0, rwkv_w2, rwkv_a0, rwkv_a2, rwkv_g2, rwkv_k_k, rwkv_k_a, rwkv_r_k,
              rwkv_lnx_w, rwkv_lnx_b, rwkv_v0, rwkv_v1, rwkv_v2,
              w_branch_a, w_branch_b, w_branch_c, w_branch_d, w_out, norm2, w_ffn_in, w_ffn_out, norm_f):
    B, S, _ = x.shape
    cos, sin = rope_tables(positions)
    c_act = jax.nn.silu(c)
    v_first = None
    for l in range(DEPTH):
        mod = c_act @ w_ada[l] + b_ada[l]
        shift1, scale1, gate1, shift2, scale2, gate2 = [m[:, None, :] for m in jnp.split(mod, 6, axis=-1)]

        h = rms_norm(x, norm1[l]) * (1.0 + scale1) + shift1
        proj = h @ w_in[l]
        p_gate, p_gla, p_dil, p_rwkv, p_moba = split_sizes(proj, IN_SIZES)
        gates = jax.nn.sigmoid(p_gate).reshape(B, S, N_BRANCH, D_MODEL)
        o_gla = gla_mixer(p_gla, gla_w_a2[l], gla_b_a2[l], gla_gnorm[l])
        o_dil = dilated_mixer(p_dil, cos, sin)
        v_mix = None if l == 0 else (rwkv_v0[l - 1], rwkv_v1[l - 1], rwkv_v2[l - 1])
        o_rwkv, v_first = rwkv7_mixer(p_rwkv, rwkv_mu[l], rwkv_w0[l], rwkv_w2[l], rwkv_a0[l], rwkv_a2[l],
                                      rwkv_g2[l], rwkv_k_k[l], rwkv_k_a[l], rwkv_r_k[l],
                                      rwkv_lnx_w[l], rwkv_lnx_b[l], v_first, v_mix)
        o_moba = moba_mixer(p_moba, cos, sin)
        merged = (gates[:, :, 0] * (o_gla @ w_branch_a[l])
                  + gates[:, :, 1] * (o_dil @ w_branch_b[l])
                  + gates[:, :, 2] * (o_rwkv @ w_branch_c[l])
                  + gates[:, :, 3] * (o_moba @ w_branch_d[l]))
        x = x + gate1 * (merged @ w_out[l])

        h2 = rms_norm(x, norm2[l]) * (1.0 + scale2) + shift2
        g_ffn, u_ffn = jnp.split(h2 @ w_ffn_in[l], 2, axis=-1)
        x = x + gate2 * ((jax.nn.silu(g_ffn) * u_ffn) @ w_ffn_out[l])
    return rms_norm(x, norm_f)
```

```python
import math
import numpy as np
from contextlib import ExitStack
import concourse.bass as bass
import concourse.mybir as mybir
from concourse.bass_utils import run_bass_kernel_spmd

F32 = mybir.dt.float32
BF16 = mybir.dt.bfloat16
I32 = mybir.dt.int32
ALU = mybir.AluOpType
AF = mybir.ActivationFunctionType
AX = mybir.AxisListType

D = 2048
NKC = 16
IN_TOTAL = 15568
FFN_H = 5632
EPS = 1e-6
O_GLA_Q, O_GLA_K, O_GLA_V, O_GLA_G, O_GLA_A = 8192, 8448, 8704, 9216, 9728
O_DIL_Q, O_DIL_K, O_DIL_V = 9744, 10512, 11280
O_RW = 12048
O_MO_Q, O_MO_K, O_MO_V = 14032, 14544, 15056
TWO_PI = 2.0 * math.pi


class Tk:
    __slots__ = ("w", "r", "dsem", "dcnt")

    def __init__(self):
        self.w = None; self.r = []; self.dsem = None; self.dcnt = 0


class FW:
    def __init__(self, nc, es):
        self.nc = nc; self.es = es; self.eng = {}
        for nm, e in (("pe", nc.tensor), ("dve", nc.vector), ("act", nc.scalar), ("pool", nc.gpsimd), ("sp", nc.sync)):
            self.eng[nm] = dict(e=e, sem=es.enter_context(nc.semaphore("c_" + nm)), n=0, waited={})
        self.allrecs = []
        self.sempool = []
        self.scope_tks = [[]]
        self.ninst = 0
        self.uid = 0
        self.es_scoped = None

    def sb(self, shape, dt=F32, name=None):
        self.uid += 1
        return (self.es_scoped or self.es).enter_context(self.nc.sbuf_tensor(name or f"sb{self.uid}", list(shape), dt))

    def ps(self, shape, dt=F32, name=None):
        self.uid += 1
        return (self.es_scoped or self.es).enter_context(self.nc.psum_tensor(name or f"ps{self.uid}", list(shape), dt))

    def _deps(self, reads, writes):
        deps = []
        for t in reads:
            if t.w is not None:
                deps.append(t.w)
        for t in writes:
            if t.w is not None:
                deps.append(t.w)
            deps.extend(t.r)
        return deps

    def _wait(self, E, deps, selfsync=False):
        best = {}
        for s, v in deps:
            k = id(s)
            if k not in best or best[k][1] < v:
                best[k] = (s, v)
        for s, v in best.values():
            if s is E["sem"] and not selfsync:
                continue
            if E["waited"].get(id(s), 0) >= v:
                continue
            E["e"].wait_ge(s, v)
            E["waited"][id(s)] = v
            self.ninst += 1

    def _mark(self, me, reads, writes):
        for t in reads:
            t.r.append(me)
        for t in writes:
            t.w = me; t.r = []

    def op(self, en, fn, reads=(), writes=(), selfsync=True):
        E = self.eng[en]
        self._wait(E, self._deps(reads, writes), selfsync and en != "pe")
        ins = fn(E["e"])
        E["n"] += 1
        ins.then_inc(E["sem"], 1)
        self.ninst += 1
        self._mark((E["sem"], E["n"]), reads, writes)

    def _rec(self, owner):
        if owner.dsem is None:
            if self.sempool:
                owner.dsem = self.sempool.pop()
            else:
                self.uid += 1
                owner.dsem = [self.es.enter_context(self.nc.semaphore(f"d{self.uid}")), 0]
                self.allrecs.append(owner.dsem)
            self.scope_tks[-1].append(owner)
        return owner.dsem

    def dma(self, en, out, in_, owner, reads=(), writes=(), **kw):
        E = self.eng[en]
        self._wait(E, self._deps(reads, writes))
        rec = self._rec(owner)
        ins = E["e"].dma_start(out=out, in_=in_, **kw)
        rec[1] += 16
        ins.then_inc(rec[0], 16)
        self.ninst += 1
        self._mark((rec[0], rec[1]), reads, writes)

    def cc(self, fn, owner, reads=(), writes=()):
        E = self.eng["pool"]
        self._wait(E, self._deps(reads, writes))
        rec = self._rec(owner)
        ins = fn(E["e"])
        rec[1] += 1
        ins.then_inc(rec[0])
        self.ninst += 1
        self._mark((rec[0], rec[1]), reads, writes)
        E["e"].wait_ge(rec[0], rec[1])
        E["waited"][id(rec[0])] = rec[1]

    def scope(self):
        fw = self

        class _S:
            def __enter__(self_):
                self_.old = fw.es_scoped
                self_.st = ExitStack()
                fw.es_scoped = self_.st
                fw.scope_tks.append([])
                return self_

            def __exit__(self_, *a):
                fw.barrier()
                for tk in fw.scope_tks.pop():
                    fw.sempool.append(tk.dsem)
                    tk.dsem = None
                self_.st.close()
                fw.es_scoped = self_.old
                return False
        return _S()

    def barrier(self):
        evs = [(E["sem"], E["n"]) for E in self.eng.values() if E["n"] > 0]
        evs += [(r[0], r[1]) for r in self.allrecs if r[1] > 0]
        for E in self.eng.values():
            self._wait(E, evs)


class Ring:
    def __init__(self, fw, shape, dt, n, psum=False):
        self.b = [((fw.ps(shape, dt) if psum else fw.sb(shape, dt)), Tk()) for _ in range(n)]
        self.i = 0

    def next(self):
        b = self.b[self.i % len(self.b)]
        self.i += 1
        return b


def build(S, L=2, W=1, stop_after=None, dbg=None):
    nc = bass.Bass("TRN2", target_bir_lowering=False)
    NT = S // 512
    ins = {}

    def din(name, shape, dt=F32):
        if name not in ins:
            ins[name] = nc.dram_tensor(name, list(shape), dt, kind="ExternalInput").ap()
        return ins[name]

    def dscr(name, shape, dt=F32):
        kind = "ExternalOutput" if (dbg and name in dbg) else "Internal"
        return nc.dram_tensor(name, list(shape), dt, kind=kind).ap()

    with ExitStack() as es:
        fw = FW(nc, es)
        cst = Tk()
        ones = fw.sb([128, 128]); ident = fw.sb([128, 128])
        fw.dma("sp", ones[:], din("c_ones", [128, 128]), cst, writes=[cst])
        fw.dma("sp", ident[:], din("c_ident", [128, 128]), cst, writes=[cst])
        invf = fw.sb([32, 1]); sgn = fw.sb([32, 1]); permT = fw.sb([32, 32])
        fw.dma("sp", invf[:], din("c_invf", [32, 1]), cst, writes=[cst])
        fw.dma("sp", sgn[:], din("c_sgn", [32, 1]), cst, writes=[cst])
        fw.dma("sp", permT[:], din("c_permT", [32, 32]), cst, writes=[cst])
        epsT = fw.sb([128, 1])
        fw.op("dve", lambda e: e.memset(epsT[:], EPS), writes=[cst])

        modT = [fw.sb([128, 96], name=f"modT{l}") for l in range(L)]
        Tmod = Tk()
        cact = fw.sb([128, 16]); Tc = Tk()
        fw.dma("sp", cact[:], din("cT", [128, 16]), Tc, writes=[Tc])
        fw.op("act", lambda e: e.activation(cact[:], cact[:], AF.Silu), reads=[Tc], writes=[Tc])
        RG = [list(range(W))]
        Twg = Tk()

        def wsrc(name, K, N, flat=False):
            if W == 1:
                return din(name, [L, K, N])
            Tb = Tk()
            full = nc.dram_tensor(name + "_g", [L, K, N], F32, kind="Internal").ap()
            if flat:
                PE_ = 128 * 2048
                npc = (K * N) // (PE_ * W)
                assert npc * PE_ * W == K * N
                sh = din(name + "_sh", [L, npc * 128, 2048])
                bn = nc.dram_tensor(name + "_b", [L, npc * 128, 2048], F32, kind="Internal").ap()
                fv = full.rearrange("l k n -> l (k n)").rearrange("l (r c) -> l r c", c=2048)
                for l_ in range(L):
                    for c_ in range(npc):
                        fw.dma("sp", bn[l_, c_ * 128:(c_ + 1) * 128, :], sh[l_, c_ * 128:(c_ + 1) * 128, :], Tb, writes=[Tb])
                for l_ in range(L):
                    for c_ in range(npc):
                        fw.cc(lambda e: e.collective_compute("AllGather", ALU.bypass, replica_groups=RG, ins=[bn[l_, c_ * 128:(c_ + 1) * 128, :]],
                                                             outs=[fv[l_, c_ * W * 128:(c_ + 1) * W * 128, :]]), Twg, reads=[Tb], writes=[Twg])
                return full
            sh = din(name + "_sh", [L, K // W, N])
            bn = nc.dram_tensor(name + "_b", [L, K // W, N], F32, kind="Internal").ap()
            rows = K // W
            RS = 128 // W
            for l_ in range(L):
                for r0 in range(0, rows, 32):
                    r1 = min(rows, r0 + 32)
                    fw.dma("sp", bn[l_, r0:r1, :], sh[l_, r0:r1, :], Tb, writes=[Tb])
            for l_ in range(L):
                for c_ in range(K // 128):
                    fw.cc(lambda e: e.collective_compute("AllGather", ALU.bypass, replica_groups=RG, ins=[bn[l_, c_ * RS:(c_ + 1) * RS, :]],
                                                         outs=[full[l_, c_ * 128:(c_ + 1) * 128, :]]), Twg, reads=[Tb], writes=[Twg])
            return full
        NCT = 96 // W
        w_ada = din("w_ada_cs", [L, D, NCT * 128]) if W > 1 else din("w_ada", [L, D, 12288])
        w_in = wsrc("w_in", D, IN_TOTAL)
        if stop_after is None or (dbg and "allw" in dbg):
            w_br = [wsrc("w_branch_a", 512, D), wsrc("w_branch_b", 256, D), wsrc("w_branch_c", 512, D), wsrc("w_branch_d", 512, D)]
            w_out = wsrc("w_out", D, D); w_f1 = wsrc("w_ffn_in", D, 2 * FFN_H, flat=True); w_f2 = wsrc("w_ffn_out", FFN_H, D)
        fw.barrier()
        if stop_after == "W":
            return nc, ins
        scM = fw.scope(); scM.__enter__()
        wring = Ring(fw, [128, NKC, 512], F32, 2)
        pmod = fw.ps([128, 96]); Tpm = Tk()
        badaT = fw.sb([128, L, 96])
        fw.dma("sp", badaT[:, :, 0:NCT], din("b_adaT", [L, 128, NCT]).rearrange("l p c -> p l c"), Tmod, writes=[Tmod])
        ncg = (NCT * 128) // 512 if (NCT * 128) % 512 == 0 else None
        for l in range(L):
            for c0 in range(0, NCT * 128, 512):
                cw = min(512, NCT * 128 - c0)
                wt, Tw = wring.next()
                fw.dma("sp", wt[:, :, 0:cw], w_ada[l, :, c0:c0 + cw].rearrange("(kc p) c -> p kc c", p=128), Tw, writes=[Tw])
                for cj in range(cw // 128):
                    ct = c0 // 128 + cj
                    for kc in range(NKC):
                        fw.op("pe", lambda e: e.matmul(pmod[:, ct:ct + 1], wt[:, kc, cj * 128:(cj + 1) * 128], cact[:, kc:kc + 1],
                                                       start=(kc == 0), stop=(kc == NKC - 1)), reads=[Tw, Tc], writes=[Tpm])
            if W == 1:
                fw.op("dve", lambda e: e.tensor_tensor(modT[l][:], pmod[:], badaT[:, l, :], ALU.add), reads=[Tpm, Tmod], writes=[Tmod])
            else:
                mloc = fw.sb([128, 16]); Tml = Tk()
                fw.op("dve", lambda e: e.memset(mloc[:], 0.0), writes=[Tml])
                fw.op("dve", lambda e: e.tensor_tensor(mloc[:, 0:NCT], pmod[:, 0:NCT], badaT[:, l, 0:NCT], ALU.add), reads=[Tpm, Tmod], writes=[Tml])
                s_mod = nc.dram_tensor(f"s_mod{l}", [128, 16], F32, kind="Internal").ap()
                g_mod = nc.dram_tensor(f"g_mod{l}", [W * 128, 16], F32, kind="Internal").ap()
                Tgm_ = Tk()
                fw.dma("pool", s_mod, mloc[:], Tml, reads=[Tml], writes=[Tgm_])
                fw.cc(lambda e: e.collective_compute("AllGather", ALU.bypass, replica_groups=RG, ins=[s_mod], outs=[g_mod]), Tgm_, reads=[Tgm_], writes=[Tgm_])
                fw.dma("sp", modT[l][:].rearrange("p (r c) -> p r c", r=W), g_mod.rearrange("(r p) c -> p r c", p=128)[:, :, 0:NCT], Tmod, reads=[Tgm_], writes=[Tmod])
        scM.__exit__(None, None, None)
        n1T = fw.sb([128, L, 16]); n2T = fw.sb([128, L, 16]); nfT = fw.sb([128, 16])
        fw.dma("sp", n1T[:], din("norm1T", [L, 128, 16]).rearrange("l p c -> p l c"), Tmod, writes=[Tmod])
        fw.dma("sp", n2T[:], din("norm2T", [L, 128, 16]).rearrange("l p c -> p l c"), Tmod, writes=[Tmod])
        fw.dma("sp", nfT[:], din("normfT", [128, 16]), Tmod, writes=[Tmod])
        g1 = [fw.sb([128, 16]) for _ in range(L)]; g2 = [fw.sb([128, 16]) for _ in range(L)]
        for l in range(L):
            fw.op("dve", lambda e: e.scalar_tensor_tensor(g1[l][:], modT[l][:, 16:32], 1.0, n1T[:, l, :], ALU.add, ALU.mult), reads=[Tmod], writes=[Tmod])
            fw.op("dve", lambda e: e.scalar_tensor_tensor(g2[l][:], modT[l][:, 64:80], 1.0, n2T[:, l, :], ALU.add, ALU.mult), reads=[Tmod], writes=[Tmod])

        xT = dscr("xT_s", [D, S])
        Txd = Tk()
        s_glaq = dscr("s_glaq", [256, S]); s_glak = dscr("s_glak", [256, S]); s_glav = dscr("s_glav", [S, 512]); s_glaa = dscr("s_glaa", [16, S])
        s_glag = dscr("s_glag", [512, S])
        s_dilq = dscr("s_dilq", [768, S]); s_dilk = dscr("s_dilk", [768, S]); s_dilv = dscr("s_dilv", [S, 1024])
        s_rw = dscr("s_rw", [1984, S])
        s_moq = dscr("s_moq", [512, S]); s_mok = dscr("s_mok", [512, S]); s_mov = dscr("s_mov", [S, 1024])
        if dbg and "s_o_in" in dbg:
            s_o = din("s_o", [1792, S])
        else:
            s_o = dscr("s_o", [1792, S])
        outT = nc.dram_tensor("outT", [D, S], F32, kind="ExternalOutput").ap()
        Tscr = Tk()
        kmeanT = fw.sb([128, 4, S // 256]); Tkm = Tk()

        xin = din("xT", [D, S])
        pos = din("pos", [1, S], I32)
        BR_ROW0 = [0, 512, 768, 1280]; BR_KC = [4, 2, 4, 4]

        class Dense:
            def __init__(s_):
                s_.xring = Ring(fw, [128, NKC, 512], F32, 1)
                s_.hring = Ring(fw, [128, NKC, 512], F32, 1)
                s_.sqring = Ring(fw, [128, 512], F32, 2)
                s_.psring = Ring(fw, [128, 512], F32, 6, psum=True)
                s_.stg = Ring(fw, [128, 512], F32, 4)
                s_.wring = Ring(fw, [128, NKC, 256], F32, 2)
                s_.rstd = fw.sb([128, 512]); s_.Trs = Tk()

            def load_x(s_, src, tt):
                xt, Tx = s_.xring.next()
                fw.dma("sp", xt[:], src[:, tt * 512:(tt + 1) * 512].rearrange("(kc p) t -> p kc t", p=128), Tx, reads=[Txd], writes=[Tx])
                return xt, Tx

            def norm(s_, xt, Tx, gvec, bias_fn):
                ht, Th = s_.hring.next()
                pp, Tp = s_.psring.next()
                for kc in range(NKC):
                    sq, Tsq = s_.sqring.next()
                    fw.op("act", lambda e: e.activation(sq[:], xt[:, kc, :], AF.Square), reads=[Tx], writes=[Tsq])
                    fw.op("pe", lambda e: e.matmul(pp[:], ones[:], sq[:], start=(kc == 0), stop=(kc == NKC - 1)), reads=[Tsq, cst], writes=[Tp])
                rstd, Trs = s_.rstd, s_.Trs
                fw.op("act", lambda e: e.activation(rstd[:], pp[:], AF.Sqrt, bias=epsT[:], scale=1.0 / D), reads=[Tp, cst], writes=[Trs])
                fw.op("dve", lambda e: e.reciprocal(rstd[:], rstd[:]), reads=[Trs], writes=[Trs])
                for kc in range(NKC):
                    sq, Tsq = s_.sqring.next()
                    fw.op("dve", lambda e: e.tensor_tensor(sq[:], xt[:, kc, :], rstd[:], ALU.mult), reads=[Tx, Trs], writes=[Tsq])
                    b = bias_fn(kc) if bias_fn is not None else 0.0
                    fw.op("act", lambda e: e.activation(ht[:, kc, :], sq[:], AF.Identity, bias=b, scale=gvec[:, kc:kc + 1]), reads=[Tsq, Tmod], writes=[Th])
                return ht, Th

            def load_w(s_, w2d, col0, cw, k0=0, nk=NKC, dstcol=0, buf=None):
                if buf is None:
                    buf = s_.wring.next()
                wt, Tw = buf
                fw.dma("sp", wt[:, 0:nk, dstcol:dstcol + cw], w2d[k0 * 128:(k0 + nk) * 128, col0:col0 + cw].rearrange("(kc p) c -> p kc c", p=128), Tw, reads=[Twg], writes=[Tw])
                return buf

        pid = nc.gpsimd.partition_id() if W > 1 else 0
        NQ = S // 256
        g_dilk_l = [[nc.dram_tensor(f"g_dilk{l_}_{h}", [(W + 1) * 128, S], F32, kind="Internal").ap() for h in range(6)] for l_ in range(L)]
        g_dilv_l = [[nc.dram_tensor(f"g_dilv{l_}_{q}", [(W + 1) * 256, 1024], F32, kind="Internal").ap() for q in range(NQ)] for l_ in range(L)]
        g_mok_l = [[nc.dram_tensor(f"g_mok{l_}_{h}", [W * 128, S], F32, kind="Internal").ap() for h in range(4)] for l_ in range(L)]
        g_mov_l = [[nc.dram_tensor(f"g_mov{l_}_{q}", [W * 256, 1024], F32, kind="Internal").ap() for q in range(NQ)] for l_ in range(L)]
        g_km_l = [nc.dram_tensor(f"g_km{l_}", [W * 128, 4 * (S // 256)], F32, kind="Internal").ap() for l_ in range(L)]
        s_km = dscr("s_km", [128, 4 * (S // 256)])
        dwv = nc.dram_tensor("dwv", [2 * S, 1024], F32, kind="Internal").ap()
        dwk = nc.dram_tensor("dwk", [6, 128, S], F32, kind="Internal").ap()
        s_dnd = dscr("s_dnd", [6, S, 129])
        Tg = Tk()
        with fw.scope():
            z = fw.sb([128, 2048]); Tz = Tk()
            fw.op("dve", lambda e: e.memset(z[:], 0.0), writes=[Tz])
            for l_ in range(L):
                for h in range(6):
                    for j in range(0, S, 2048):
                        fw.dma("pool", g_dilk_l[l_][h][0:128, j:j + 2048], z[:], Tz, reads=[Tz], writes=[Tg])
                for q in range(NQ):
                    for i in range(0, 256, 128):
                        fw.dma("pool", g_dilv_l[l_][q][i:i + 128, :], z[:, 0:1024], Tz, reads=[Tz], writes=[Tg])

        def allgather(src, dst_ap):
            if W == 1:
                fw.dma("pool", dst_ap, src, Tg, reads=[Tscr, Tg], writes=[Tg])
            else:
                fw.cc(lambda e: e.collective_compute("AllGather", ALU.bypass, replica_groups=RG, ins=[src], outs=[dst_ap]), Tg, reads=[Tscr, Tg], writes=[Tg])

        dmask = fw.sb([128, 256]); tri = dmask[:, 128:256]
        fw.dma("sp", dmask[:], din("c_dmask", [128, 256]), cst, writes=[cst])
        mo_cb = din("mo_cb", [S // 256, 128, 64]); mo_vm = din("mo_vm", [S // 256, 128, 64])
        c_esel = din("c_esel", [64, 64 * 128])
        NBL = S // 256
        NBG = W * NBL

        def mixers_attn(l):
            g_dilk, g_dilv, g_mok, g_mov, g_km = g_dilk_l[l], g_dilv_l[l], g_mok_l[l], g_mov_l[l], g_km_l[l]
            fw.barrier()
            fw.dma("pool", s_km, kmeanT[:].rearrange("p h b -> p (h b)"), Tkm, reads=[Tkm], writes=[Tscr])
            fw.barrier()
            for h in range(6):
                allgather(s_dilk[h * 128:(h + 1) * 128, :], g_dilk[h][128:, :])
            for q in range(NQ):
                allgather(s_dilv[q * 256:(q + 1) * 256, :].rearrange("(a b) c -> a (b c)", b=2), g_dilv[q][256:, :].rearrange("(a b) c -> a (b c)", b=2))
                allgather(s_mov[q * 256:(q + 1) * 256, :].rearrange("(a b) c -> a (b c)", b=2), g_mov[q].rearrange("(a b) c -> a (b c)", b=2))
            for h in range(4):
                allgather(s_mok[h * 128:(h + 1) * 128, :], g_mok[h])
            allgather(s_km, g_km)
            fw.barrier()
            for q in range(NQ):
                fw.dma("pool", dwv[q * 256:(q + 1) * 256, :], g_dilv[q][bass.ds(pid * 256, 256), :], Tg, reads=[Tg], writes=[Tscr])
            fw.dma("pool", dwv[S:2 * S, :], s_dilv, Tg, reads=[Tg, Tscr], writes=[Tscr])
            for h in range(6):
                fw.dma("pool", dwk[h], g_dilk[h][bass.ds(pid * 128, 128), :], Tg, reads=[Tg], writes=[Tscr])
            fw.barrier()
            if dbg and "noDil" in dbg:
                pass
            else:
              with fw.scope():
                  KT = fw.sb([128, 2 * S]); QT = fw.sb([128, S]); Tkq = Tk()
                  psS = Ring(fw, [128, 256], F32, 2, psum=True)
                  psO = Ring(fw, [128, 129], F32, 2, psum=True)
                  pT = Ring(fw, [128, 256], F32, 3)
                  vt = Ring(fw, [128, 2, 129], F32, 3)
                  nd = Ring(fw, [128, 129], F32, 3)
                  for h in range(6):
                      dl = (1, 4, 16)[h // 2]
                      fw.dma("sp", KT[:, 0:S], dwk[h], Tkq, reads=[Tscr], writes=[Tkq])
                      fw.dma("sp", KT[:, S:2 * S], s_dilk[h * 128:(h + 1) * 128, :], Tkq, reads=[Tscr], writes=[Tkq])
                      fw.dma("sp", QT[:], s_dilq[h * 128:(h + 1) * 128, :], Tkq, reads=[Tscr], writes=[Tkq])
                      nbl = S // (dl * 128)
                      for r in range(dl):
                          for nb in range(nbl):
                              q0 = r + dl * 128 * nb
                              k0 = S + r + dl * 128 * (nb - 1)
                              ps, Tps = psS.next()
                              for j in range(2):
                                  kk0 = k0 + j * dl * 128
                                  fw.op("pe", lambda e: e.matmul(ps[:, j * 128:(j + 1) * 128], KT[:, kk0:kk0 + dl * 127 + 1:dl], QT[:, q0:q0 + dl * 127 + 1:dl],
                                                                 start=True, stop=True), reads=[Tkq], writes=[Tps])
                              p_, Tp_ = pT.next()
                              fw.op("act", lambda e: e.activation(p_[:], ps[:], AF.Exp), reads=[Tps], writes=[Tp_])
                              fw.op("dve", lambda e: e.tensor_tensor(p_[:], p_[:], dmask[:], ALU.mult), reads=[cst], writes=[Tp_])
                              v_, Tv_ = vt.next()
                              vr0 = k0
                              fw.dma("sp", v_[:], dwv[vr0:vr0 + 255 * dl + 1:dl, h * 129:(h + 1) * 129].rearrange("(j p) e -> p j e", p=128), Tv_, reads=[Tscr], writes=[Tv_])
                              po, Tpo = psO.next()
                              for j in range(2):
                                  fw.op("pe", lambda e: e.matmul(po[:], p_[:, j * 128:(j + 1) * 128], v_[:, j, :], start=(j == 0), stop=(j == 1)),
                                        reads=[Tp_, Tv_], writes=[Tpo])
                              n_, Tn_ = nd.next()
                              fw.op("act", lambda e: e.copy(n_[:], po[:]), reads=[Tpo], writes=[Tn_])
                              fw.dma("pool", s_dnd[h, q0:q0 + dl * 127 + 1:dl, :], n_[:], Tn_, reads=[Tn_], writes=[Tscr])
                  fw.barrier()
                  cb = Ring(fw, [128, 3, 129], F32, 2)
                  ot = Ring(fw, [128, 128], F32, 2)
                  ostg = Ring(fw, [128, 512], F32, 2)
                  rc = fw.sb([128, 1]); Trc = Tk()
                  psT = Ring(fw, [128, 128], F32, 2, psum=True)
                  for hp in range(2):
                      for t4 in range(S // 512):
                          og, Tog = ostg.next()
                          for ts in range(4):
                              r0 = t4 * 512 + ts * 128
                              c_, Tc_ = cb.next()
                              for g in range(3):
                                  fw.dma("sp", c_[:, g, :], s_dnd[g * 2 + hp, r0:r0 + 128, :], Tc_, reads=[Tscr], writes=[Tc_])
                              fw.op("dve", lambda e: e.tensor_tensor(c_[:, 0, :], c_[:, 0, :], c_[:, 1, :], ALU.add), writes=[Tc_])
                              fw.op("dve", lambda e: e.tensor_tensor(c_[:, 0, :], c_[:, 0, :], c_[:, 2, :], ALU.add), writes=[Tc_])
                              fw.op("dve", lambda e: e.reciprocal(rc[:], c_[:, 0, 128:129]), reads=[Tc_], writes=[Trc])
                              o_, To_ = ot.next()
                              fw.op("act", lambda e: e.activation(o_[:], c_[:, 0, 0:128], AF.Copy, scale=rc[:, 0:1]), reads=[Tc_, Trc], writes=[To_])
                              pt, Tpt = psT.next()
                              fw.op("pe", lambda e: e.transpose(pt[:], o_[:], ident[:]), reads=[To_, cst], writes=[Tpt])
                              fw.op("act", lambda e: e.copy(og[:, ts * 128:(ts + 1) * 128], pt[:]), reads=[Tpt], writes=[Tog])
                          fw.dma("pool", s_o[512 + hp * 128:512 + (hp + 1) * 128, t4 * 512:(t4 + 1) * 512], og[:], Tog, reads=[Tog], writes=[Tscr])
            if dbg and "noMoba" in dbg:
                return
            with fw.scope():
                KTh = fw.sb([128, W * S]); Vh = fw.sb([128, W * S // 128, 129]); QTm = fw.sb([128, S]); Tkv = Tk()
                esel = fw.sb([64, 64 * 128]); Tes = Tk()
                fw.dma("sp", esel[:], c_esel, Tes, writes=[Tes])
                kmT = fw.sb([128, W, 4 * NBL]); kmh = fw.sb([128, 64]); Tkmh = Tk()
                fw.op("dve", lambda e: e.memset(kmh[:], 0.0), writes=[Tkmh])
                fw.dma("sp", kmT[:], g_km.rearrange("(r p) c -> p r c", p=128), Tkmh, reads=[Tg], writes=[Tkmh])
                cbv = fw.sb([128, 2, 64]); Tcb = Tk()
                gm = fw.sb([128, 64]); mx = fw.sb([128, 8]); Tgm = Tk()
                mbT = fw.sb([64, 256]); Tmb = Tk()
                kown = fw.sb([128, 256]); vown = fw.sb([128, 2, 129]); Town = Tk()
                psS = Ring(fw, [128, 384], F32, 2, psum=True)
                psG = Ring(fw, [128, 128], F32, 2, psum=True)
                psO = [fw.ps([128, 129]), fw.ps([128, 129])]; TpO = [Tk(), Tk()]
                pT = Ring(fw, [128, 384], F32, 3)
                ot = Ring(fw, [128, 128], F32, 2)
                ostg = Ring(fw, [128, 256], F32, 2)
                rc = fw.sb([128, 1]); Trc = Tk()
                for hd in range(4):
                    for r_ in range(W):
                        fw.dma("sp", KTh[:, r_ * S:(r_ + 1) * S], g_mok[hd][r_ * 128:(r_ + 1) * 128, :], Tkv, reads=[Tg], writes=[Tkv])
                    Vh5 = Vh[:].rearrange("p (r q j) e -> p r q j e", r=W, q=NQ)
                    for q in range(NQ):
                        for r_ in range(W):
                            fw.dma("sp", Vh5[:, r_, q, :, :], g_mov[q][r_ * 256:(r_ + 1) * 256, hd * 129:(hd + 1) * 129].rearrange("(j p) e -> p j e", p=128), Tkv, reads=[Tg], writes=[Tkv])
                    fw.dma("sp", QTm[:], s_moq[hd * 128:(hd + 1) * 128, :], Tkv, reads=[Tscr], writes=[Tkv])
                    fw.op("dve", lambda e: e.tensor_scalar(kmh[:, 0:NBG].rearrange("p (r b) -> p r b", r=W), kmT[:, :, hd * NBL:(hd + 1) * NBL], 1.0 / 256, None, ALU.mult),
                          writes=[Tkmh])
                    for QB in range(NBL):
                        fw.dma("sp", cbv[:, 0, :], mo_cb[QB], Tcb, writes=[Tcb])
                        fw.dma("sp", cbv[:, 1, :], mo_vm[QB], Tcb, writes=[Tcb])
                        for ch in range(2):
                            qc = QB * 256 + ch * 128
                            pg, Tpg = psG.next()
                            fw.op("pe", lambda e: e.matmul(pg[:, 0:64], QTm[:, qc:qc + 128], kmh[:], start=True, stop=True), reads=[Tkv, Tkmh], writes=[Tpg])
                            fw.op("dve", lambda e: e.tensor_tensor(gm[:], pg[:, 0:64], cbv[:, 0, :], ALU.add), reads=[Tpg, Tcb], writes=[Tgm])
                            fw.op("dve", lambda e: e.max(mx[:], gm[:]), writes=[Tgm])
                            fw.op("pool", lambda e: e.tensor_scalar(gm[:], gm[:], mx[:, 2:3], None, ALU.is_ge), reads=[Tgm], writes=[Tgm])
                            fw.op("dve", lambda e: e.tensor_tensor(gm[:], gm[:], cbv[:, 1, :], ALU.mult), reads=[Tcb], writes=[Tgm])
                            fw.op("dve", lambda e: e.tensor_scalar(gm[:], gm[:], -30000.0, None, ALU.add), writes=[Tgm])
                            pt, Tpt = psG.next()
                            fw.op("pe", lambda e: e.transpose(pt[0:64, :], gm[:], ident[:]), reads=[Tgm, cst], writes=[Tpt])
                            fw.op("act", lambda e: e.copy(mbT[:, ch * 128:(ch + 1) * 128], pt[0:64, :]), reads=[Tpt], writes=[Tmb])
                        first = [True, True]
                        for n in range(NBG):
                            for hf in range(2):
                                kc0 = n * 256 + hf * 128
                                ps, Tps = psS.next()
                                fw.op("pe", lambda e: e.matmul(ps[:, 0:256], esel[:, n * 128:(n + 1) * 128], mbT[:], start=True, stop=False), reads=[Tes, Tmb], writes=[Tps])
                                fw.op("pe", lambda e: e.matmul(ps[:, 0:256], KTh[:, kc0:kc0 + 128], QTm[:, QB * 256:(QB + 1) * 256], start=False, stop=True), reads=[Tkv], writes=[Tps])
                                p_, Tp_ = pT.next()
                                fw.op("act", lambda e: e.activation(p_[:, 0:256], ps[:, 0:256], AF.Exp), reads=[Tps], writes=[Tp_])
                                for ch in range(2):
                                    fw.op("pe", lambda e: e.matmul(psO[ch][:], p_[:, ch * 128:(ch + 1) * 128], Vh[:, n * 2 + hf, :], start=first[ch], stop=False),
                                          reads=[Tp_, Tkv], writes=[TpO[ch]])
                                    first[ch] = False
                        fw.dma("sp", kown[:], s_mok[hd * 128:(hd + 1) * 128, QB * 256:(QB + 1) * 256], Town, reads=[Tscr], writes=[Town])
                        fw.dma("sp", vown[:], s_mov[QB * 256:(QB + 1) * 256, hd * 129:(hd + 1) * 129].rearrange("(j p) e -> p j e", p=128), Town, reads=[Tscr], writes=[Town])
                        ps, Tps = psS.next()
                        combos = ((0, 0), (0, 1), (1, 1))
                        for ci, (hf, ch) in enumerate(combos):
                            fw.op("pe", lambda e: e.matmul(ps[:, ci * 128:(ci + 1) * 128], kown[:, hf * 128:(hf + 1) * 128], QTm[:, QB * 256 + ch * 128:QB * 256 + (ch + 1) * 128],
                                                           start=True, stop=True), reads=[Town, Tkv], writes=[Tps])
                        p_, Tp_ = pT.next()
                        fw.op("act", lambda e: e.activation(p_[:], ps[:], AF.Exp), reads=[Tps], writes=[Tp_])
                        fw.op("dve", lambda e: e.tensor_tensor(p_[:, 0:128], p_[:, 0:128], tri, ALU.mult), reads=[cst], writes=[Tp_])
                        fw.op("dve", lambda e: e.tensor_tensor(p_[:, 256:384], p_[:, 256:384], tri, ALU.mult), reads=[cst], writes=[Tp_])
                        lastci = {0: 0, 1: 2}
                        for ci, (hf, ch) in enumerate(combos):
                            fw.op("pe", lambda e: e.matmul(psO[ch][:], p_[:, ci * 128:(ci + 1) * 128], vown[:, hf, :], start=first[ch], stop=(ci == lastci[ch])),
                                  reads=[Tp_, Town], writes=[TpO[ch]])
                            first[ch] = False
                        og, Tog = ostg.next()
                        for ch in range(2):
                            fw.op("dve", lambda e: e.reciprocal(rc[:], psO[ch][:, 128:129]), reads=[TpO[ch]], writes=[Trc])
                            o_, To_ = ot.next()
                            fw.op("act", lambda e: e.activation(o_[:], psO[ch][:, 0:128], AF.Copy, scale=rc[:, 0:1]), reads=[TpO[ch], Trc], writes=[To_])
                            pt, Tpt = psG.next()
                            fw.op("pe", lambda e: e.transpose(pt[:], o_[:], ident[:]), reads=[To_, cst], writes=[Tpt])
                            fw.op("act", lambda e: e.copy(og[:, ch * 128:(ch + 1) * 128], pt[:]), reads=[Tpt], writes=[Tog])
                        fw.dma("pool", s_o[1280 + hd * 128:1280 + (hd + 1) * 128, QB * 256:(QB + 1) * 256], og[:], Tog, reads=[Tog], writes=[Tscr])

        C_LD = -0.6065306597126334
        s_rr = dscr("s_rr", [512, S]); s_rk = dscr("s_rk", [512, S]); s_rld = dscr("s_rld", [512, S])
        s_ra = dscr("s_ra", [512, S]); s_rb = dscr("s_rb", [512, S]); s_rv = dscr("s_rv", [S, 512])
        s_rg = dscr("s_rg", [512, S]); s_rbv = dscr("s_rbv", [512, S]); s_vf = dscr("s_vf", [512, S])
        s_gld = dscr("s_gld", [256, S]); s_graw = dscr("s_graw", [512, S])
        s_halo = dscr("s_halo", [128, 16])
        g_halo_l = [nc.dram_tensor(f"g_halo{l_}", [W * 128, 16], F32, kind="Internal").ap() for l_ in range(L)]
        NU = 16
        blk64 = fw.sb([128, 128]); maskSR = fw.sb([64, 512]); maskSL = fw.sb([64, 512]); s0aug = fw.sb([64, 128]); selv = fw.sb([64, W])
        fw.dma("sp", blk64[:], din("c_blk64", [128, 128]), cst, writes=[cst])
        fw.dma("sp", maskSR[:], din("c_maskSR", [64, 512]), cst, writes=[cst])
        fw.dma("sp", maskSL[:], din("c_maskSL", [64, 512]), cst, writes=[cst])
        fw.dma("sp", s0aug[:], din("c_s0aug", [64, 128]), cst, writes=[cst])
        fw.dma("sp", selv[:], din("c_selv", [64, W]), cst, writes=[cst])
        selp = fw.sb([128, W])
        fw.dma("sp", selp[:], din("c_selp", [128, W]), cst, writes=[cst])
        RW_GROUPS = [(i * 128, 128) for i in range(12)] + [(1536, 96), (1632, 96), (1728, 128), (1856, 128)]

        def scan_pre(l):
            g_halo = g_halo_l[l]
            fw.barrier()
            for gi_, (r0_, n_) in enumerate(RW_GROUPS):
                fw.dma("pool", s_halo[0:n_, gi_:gi_ + 1], s_rw[r0_:r0_ + n_, S - 1:S], Tg, reads=[Tscr, Tg], writes=[Tg], allow_slow_non_contiguous=True)
            fw.barrier()
            allgather(s_halo, g_halo)
            fw.barrier()
            with fw.scope():
                prm = Tk()

                def ldp(name, shape, src_ap):
                    t = fw.sb(shape)
                    fw.dma("sp", t[:], src_ap, prm, writes=[prm])
                    return t
                muG = ldp("mu", [128, 16], din("rwkv_muG", [L, 128, 16])[l])
                w0T = ldp("w0", [128, 4], din("rwkv_w0T", [L, 128, 4])[l]); a0T = ldp("a0", [128, 4], din("rwkv_a0T", [L, 128, 4])[l])
                kkT = ldp("kk", [128, 4], din("rwkv_k_kT", [L, 128, 4])[l]); kaT = ldp("ka", [128, 4], din("rwkv_k_aT", [L, 128, 4])[l])
                rkT = ldp("rk", [128, 4], din("rwkv_r_kT", [L, 128, 4])[l])
                w2s = ldp("w2", [96, 512], din("rwkv_w2", [L, 96, 512])[l]); a2s = ldp("a2", [96, 512], din("rwkv_a2", [L, 96, 512])[l])
                g2s = ldp("g2", [128, 2, 512], din("rwkv_g2", [L, 256, 512])[l].rearrange("(kc p) c -> p kc c", p=128))
                if l > 0:
                    v0T = ldp("v0", [128, 4], din("rwkv_v0T", [L - 1, 128, 4])[l - 1])
                    v1s = ldp("v1", [128, 4, 64], din("rwkv_v1", [L - 1, 512, 64])[l - 1].rearrange("(kc p) c -> p kc c", p=128))
                    v2s = ldp("v2", [64, 512], din("rwkv_v2", [L - 1, 64, 512])[l - 1])
                wa2 = ldp("wa2", [16, 256], din("gla_w_a2", [L, 16, 256])[l]); ba2 = ldp("ba2", [128, 2], din("gla_b_a2T", [L, 128, 2])[l])
                omka = fw.sb([128, 4]); nba2 = fw.sb([128, 2])
                fw.op("dve", lambda e: e.tensor_scalar(omka[:], kaT[:], -1.0, 1.0, ALU.mult, ALU.add), reads=[prm], writes=[prm])
                fw.op("dve", lambda e: e.tensor_scalar(nba2[:], ba2[:], -1.0, None, ALU.mult), reads=[prm], writes=[prm])
                XS = fw.sb([128, 16, 512]); Txs = Tk()
                p513 = Ring(fw, [128, 513], F32, 3)
                tmp = Ring(fw, [128, 512], F32, 6)
                ostg = Ring(fw, [128, 512], F32, 4)
                psr = Ring(fw, [128, 512], F32, 4, psum=True)
                t64 = fw.sb([64, 512]); Tt64 = Tk()
                hl = fw.sb([128, W, 16]); Thl = Tk(); hl2 = fw.sb([128, W]); Thl2 = Tk()
                wl_t = (fw.sb([128, 512]), Tk()); gs_t = [(fw.sb([128, 512]), Tk()), (fw.sb([128, 512]), Tk())]
                al = fw.sb([16, 512]); Tal = Tk()

                def store(dst, rows0, tile_ap, Tt, t0, nrows=128):
                    fw.dma("pool", dst[rows0:rows0 + nrows, t0:t0 + 512], tile_ap, Tt, reads=[Tt], writes=[Tscr])

                for tt in range(NT):
                    t0 = tt * 512
                    for gi, (r0, n) in enumerate(RW_GROUPS):
                        p_, Tp_ = p513.next()
                        fw.dma("sp", p_[0:n, 1:513], s_rw[r0:r0 + n, t0:t0 + 512], Tp_, reads=[Tscr], writes=[Tp_])
                        if tt > 0:
                            fw.dma("sp", p_[0:n, 0:1], s_rw[r0:r0 + n, t0 - 1:t0], Tp_, reads=[Tscr], writes=[Tp_], allow_slow_non_contiguous=True)
                        else:
                            if gi == 0:
                                fw.dma("sp", hl[:], g_halo.rearrange("(r p) c -> p r c", p=128), Thl, reads=[Tg], writes=[Thl])
                            fw.op("dve", lambda e: e.tensor_tensor(hl2[0:n, :], hl[0:n, :, gi], selp[0:n, :], ALU.mult), reads=[Thl, cst], writes=[Thl2])
                            fw.op("dve", lambda e: e.tensor_reduce(p_[0:n, 0:1], hl2[0:n, :], AX.X, ALU.add), reads=[Thl2], writes=[Tp_])
                        d_, Td_ = tmp.next()
                        fw.op("dve", lambda e: e.tensor_tensor(d_[0:n, :], p_[0:n, 0:512], p_[0:n, 1:513], ALU.subtract), reads=[Tp_], writes=[Td_])
                        fw.op("dve", lambda e: e.scalar_tensor_tensor(XS[0:n, gi, :], d_[0:n, :], muG[0:n, gi:gi + 1], p_[0:n, 1:513], ALU.mult, ALU.add),
                              reads=[Td_, Tp_, prm], writes=[Txs])
                    wl, Twl = wl_t
                    fw.op("act", lambda e: e.activation(wl[0:96, :], XS[0:96, 12, :], AF.Tanh), reads=[Txs], writes=[Twl])
                    gs = gs_t
                    for j in range(2):
                        fw.op("act", lambda e: e.activation(gs[j][0][:], XS[:, 14 + j, :], AF.Sigmoid), reads=[Txs], writes=[gs[j][1]])
                    if l > 0:
                        pv, Tpv = psr.next()
                        for kc in range(4):
                            fw.op("pe", lambda e: e.matmul(pv[0:64, :], v1s[:, kc, :], XS[:, 8 + kc, :], start=(kc == 0), stop=(kc == 3)), reads=[prm, Txs], writes=[Tpv])
                        fw.op("act", lambda e: e.copy(t64[:], pv[0:64, :]), reads=[Tpv], writes=[Tt64])
                    for m in range(4):
                        cs = slice(m * 128, (m + 1) * 128)
                        pz, Tpz = psr.next()
                        fw.op("pe", lambda e: e.matmul(pz[:], w2s[:, cs], wl[0:96, :], start=True, stop=True), reads=[prm, Twl], writes=[Tpz])
                        o1, To1 = ostg.next()
                        fw.op("act", lambda e: e.activation(o1[:], pz[:], AF.Sigmoid, bias=w0T[:, m:m + 1]), reads=[Tpz, prm], writes=[To1])
                        fw.op("dve", lambda e: e.tensor_scalar(o1[:], o1[:], C_LD, None, ALU.mult), writes=[To1])
                        store(s_rld, m * 128, o1[:], To1, t0)
                        pa, Tpa = psr.next()
                        fw.op("pe", lambda e: e.matmul(pa[:], a2s[:, cs], XS[0:96, 13, :], start=True, stop=True), reads=[prm, Txs], writes=[Tpa])
                        a_, Ta_ = tmp.next()
                        fw.op("act", lambda e: e.activation(a_[:], pa[:], AF.Sigmoid, bias=a0T[:, m:m + 1]), reads=[Tpa, prm], writes=[Ta_])
                        pg_, Tpg_ = psr.next()
                        for kc in range(2):
                            fw.op("pe", lambda e: e.matmul(pg_[:], g2s[:, kc, cs], gs[kc][0][:], start=(kc == 0), stop=(kc == 1)), reads=[prm, gs[kc][1]], writes=[Tpg_])
                        o2, To2 = ostg.next()
                        fw.op("act", lambda e: e.copy(o2[:], pg_[:]), reads=[Tpg_], writes=[To2])
                        store(s_rg, m * 128, o2[:], To2, t0)
                        v_, Tv_ = tmp.next()
                        if l == 0:
                            fw.op("act", lambda e: e.copy(v_[:], XS[:, 8 + m, :]), reads=[Txs], writes=[Tv_])
                            fw.dma("pool", s_vf[cs, t0:t0 + 512], v_[:], Tv_, reads=[Tv_], writes=[Tscr])
                        else:
                            pm, Tpm_ = psr.next()
                            fw.op("pe", lambda e: e.matmul(pm[:], v2s[:, cs], t64[:], start=True, stop=True), reads=[prm, Tt64], writes=[Tpm_])
                            sg_, Tsg_ = tmp.next()
                            fw.op("act", lambda e: e.activation(sg_[:], pm[:], AF.Sigmoid, bias=v0T[:, m:m + 1]), reads=[Tpm_, prm], writes=[Tsg_])
                            fw.dma("sp", v_[:], s_vf[cs, t0:t0 + 512], Tv_, reads=[Tscr], writes=[Tv_])
                            fw.op("dve", lambda e: e.tensor_tensor(v_[:], v_[:], XS[:, 8 + m, :], ALU.subtract), reads=[Txs], writes=[Tv_])
                            fw.op("dve", lambda e: e.tensor_tensor(v_[:], v_[:], sg_[:], ALU.mult), reads=[Tsg_], writes=[Tv_])
                            fw.op("dve", lambda e: e.tensor_tensor(v_[:], v_[:], XS[:, 8 + m, :], ALU.add), reads=[Txs], writes=[Tv_])
                        pt_, Tpt_ = psr.next()
                        for ts in range(4):
                            fw.op("pe", lambda e: e.transpose(pt_[:, ts * 128:(ts + 1) * 128], v_[:, ts * 128:(ts + 1) * 128], ident[:]), reads=[Tv_, cst], writes=[Tpt_])
                        o3, To3 = ostg.next()
                        fw.op("act", lambda e: e.copy(o3[:], pt_[:]), reads=[Tpt_], writes=[To3])
                        for ts in range(4):
                            fw.dma("pool", s_rv[t0 + ts * 128:t0 + (ts + 1) * 128, cs], o3[:, ts * 128:(ts + 1) * 128], To3, reads=[To3], writes=[Tscr])
                        k_, Tk_ = tmp.next()
                        fw.op("dve", lambda e: e.tensor_scalar(k_[:], XS[:, 4 + m, :], kkT[:, m:m + 1], None, ALU.mult), reads=[Txs, prm], writes=[Tk_])
                        sq_, Tsq_ = tmp.next()
                        fw.op("act", lambda e: e.activation(sq_[:], k_[:], AF.Square), reads=[Tk_], writes=[Tsq_])
                        pn, Tpn = psr.next()
                        fw.op("pe", lambda e: e.matmul(pn[:], blk64[:], sq_[:], start=True, stop=True), reads=[cst, Tsq_], writes=[Tpn])
                        fw.op("act", lambda e: e.activation(sq_[:], pn[:], AF.Sqrt), reads=[Tpn], writes=[Tsq_])
                        fw.op("dve", lambda e: e.tensor_scalar(sq_[:], sq_[:], 1e-12, None, ALU.max), writes=[Tsq_])
                        fw.op("dve", lambda e: e.reciprocal(sq_[:], sq_[:]), writes=[Tsq_])
                        fw.op("dve", lambda e: e.tensor_tensor(k_[:], k_[:], sq_[:], ALU.mult), reads=[Tsq_], writes=[Tk_])
                        o4, To4 = ostg.next()
                        fw.op("dve", lambda e: e.tensor_tensor(o4[:], k_[:], a_[:], ALU.mult), reads=[Tk_, Ta_], writes=[To4])
                        store(s_rb, m * 128, o4[:], To4, t0)
                        o5, To5 = ostg.next()
                        fw.op("dve", lambda e: e.tensor_scalar(o5[:], k_[:], -1.0, None, ALU.mult), reads=[Tk_], writes=[To5])
                        store(s_ra, m * 128, o5[:], To5, t0)
                        fw.op("dve", lambda e: e.tensor_scalar(a_[:], a_[:], kaT[:, m:m + 1], omka[:, m:m + 1], ALU.mult, ALU.add), reads=[prm], writes=[Ta_])
                        o6, To6 = ostg.next()
                        fw.op("dve", lambda e: e.tensor_tensor(o6[:], XS[:, 4 + m, :], a_[:], ALU.mult), reads=[Txs, Ta_], writes=[To6])
                        store(s_rk, m * 128, o6[:], To6, t0)
                        fw.dma("pool", s_rr[cs, t0:t0 + 512], XS[:, m, :], Txs, reads=[Txs], writes=[Tscr])
                        fw.op("dve", lambda e: e.scalar_tensor_tensor(sq_[:], XS[:, m, :], rkT[:, m:m + 1], o6[:], ALU.mult, ALU.mult), reads=[Txs, To6, prm], writes=[Tsq_])
                        pb_, Tpb_ = psr.next()
                        fw.op("pe", lambda e: e.matmul(pb_[:], blk64[:], sq_[:], start=True, stop=True), reads=[cst, Tsq_], writes=[Tpb_])
                        o7, To7 = ostg.next()
                        fw.op("dve", lambda e: e.tensor_tensor(o7[:], pb_[:], v_[:], ALU.mult), reads=[Tpb_, Tv_], writes=[To7])
                        store(s_rbv, m * 128, o7[:], To7, t0)
                    fw.dma("sp", al[:], s_glaa[:, t0:t0 + 512], Tal, reads=[Tscr], writes=[Tal])
                    for m in range(2):
                        pq, Tpq = psr.next()
                        fw.op("pe", lambda e: e.matmul(pq[:], wa2[:, m * 128:(m + 1) * 128], al[:], start=True, stop=True), reads=[prm, Tal], writes=[Tpq])
                        o8, To8 = ostg.next()
                        fw.op("act", lambda e: e.activation(o8[:], pq[:], AF.Exp, bias=nba2[:, m:m + 1], scale=-1.0), reads=[Tpq, prm], writes=[To8])
                        fw.op("act", lambda e: e.activation(o8[:], o8[:], AF.Ln, bias=1.0), writes=[To8])
                        fw.op("dve", lambda e: e.tensor_scalar(o8[:], o8[:], -1.0 / 16, None, ALU.mult), writes=[To8])
                        store(s_gld, m * 128, o8[:], To8, t0)

        NCH = S // 64

        def scan_unit_pre(delta, srcs, rscale):
            U = {}
            T_ = Tk(); U["T"] = T_
            AR = fw.sb([64, NCH, 2, 64]); eL = fw.sb([64, S]); U["AR"] = AR; U["eL"] = eL
            Kt = fw.sb([64, NCH, 64]); U["Kt"] = Kt
            KM = fw.sb([64, NCH, 2, 64]); U["KM"] = KM
            if delta:
                Bt = fw.sb([64, NCH, 64]); U["Bt"] = Bt
                BM = fw.sb([64, NCH, 2, 64]); U["BM"] = BM
                NT_ = fw.sb([64, NCH, 64])
                Tt = fw.sb([64, NCH, 64]); Tn = fw.sb([64, NCH, 64])
            scA = fw.scope(); scA.__enter__()
            raw = {}
            for nm in (("r", "k", "ld", "a", "b") if delta else ("r", "k", "ld")):
                raw[nm] = fw.sb([64, S])
                fw.dma("sp", raw[nm][:], srcs[nm], T_, reads=[Tscr], writes=[T_])
            Lc = fw.sb([64, S]); enL = fw.sb([64, S])
            for c in range(NCH):
                fw.op("dve", lambda e: e.tensor_tensor_scan(Lc[:, c * 64:(c + 1) * 64], raw["ld"][:, c * 64:(c + 1) * 64], raw["ld"][:, c * 64:(c + 1) * 64], 0.0, ALU.add, ALU.bypass),
                      reads=[T_], writes=[T_])
            fw.op("act", lambda e: e.activation(eL[:], Lc[:], AF.Exp), reads=[T_], writes=[T_])
            fw.op("act", lambda e: e.activation(enL[:], Lc[:], AF.Exp, scale=-1.0), reads=[T_], writes=[T_])
            kt_ = raw["k"]
            fw.op("dve", lambda e: e.scalar_tensor_tensor(AR[:, :, 1, :], raw["r"][:].rearrange("p (c t) -> p c t", t=64), rscale, eL[:].rearrange("p (c t) -> p c t", t=64), ALU.mult, ALU.mult),
                  reads=[T_], writes=[T_])
            fw.op("dve", lambda e: e.tensor_tensor(kt_[:], raw["k"][:], enL[:], ALU.mult), reads=[T_], writes=[T_])
            if delta:
                bt_ = raw["b"]
                fw.op("dve", lambda e: e.tensor_tensor(bt_[:], raw["b"][:], enL[:], ALU.mult), reads=[T_], writes=[T_])
                fw.op("dve", lambda e: e.tensor_tensor(Lc[:], Lc[:], raw["ld"][:], ALU.subtract), reads=[T_], writes=[T_])
                fw.op("act", lambda e: e.activation(Lc[:], Lc[:], AF.Exp), reads=[T_], writes=[T_])
                fw.op("dve", lambda e: e.tensor_tensor(AR[:, :, 0, :], raw["a"][:].rearrange("p (c t) -> p c t", t=64), Lc[:].rearrange("p (c t) -> p c t", t=64), ALU.mult),
                      reads=[T_], writes=[T_])
            else:
                fw.op("dve", lambda e: e.memset(AR[:, :, 0, :], 0.0), reads=[T_], writes=[T_])
            ps4 = Ring(fw, [64, 512], F32, 4, psum=True)
            tl = [(kt_, Kt)]
            if delta:
                tl.append((bt_, Bt))
            for src_, dst_ in tl:
                for c8 in range(0, NCH, 8):
                    pp, Tp = ps4.next()
                    for c in range(8):
                        fw.op("pe", lambda e: e.transpose(pp[:, c * 64:(c + 1) * 64], src_[:, (c8 + c) * 64:(c8 + c + 1) * 64], ident[0:64, 0:64]), reads=[T_, cst], writes=[Tp])
                    fw.op("act", lambda e: e.copy(dst_[:, c8:c8 + 8, :].rearrange("p c j -> p (c j)"), pp[:]), reads=[Tp], writes=[T_])
            for c4 in range(0, NCH, 4):
                pp, Tp = ps4.next()
                for c in range(4):
                    fw.op("pe", lambda e: e.matmul(pp[:, c * 128:(c + 1) * 128], kt_[:, (c4 + c) * 64:(c4 + c + 1) * 64], AR[:, c4 + c, :, :].rearrange("p a t -> p (a t)"), start=True, stop=True),
                          reads=[T_], writes=[Tp])
                fw.op("dve", lambda e: e.tensor_tensor(KM[:, c4:c4 + 4, :, :].rearrange("p c a t -> p (c a t)"), pp[:], maskSR[:], ALU.mult), reads=[Tp, cst], writes=[T_])
                if delta:
                    pp, Tp = ps4.next()
                    for c in range(4):
                        fw.op("pe", lambda e: e.matmul(pp[:, c * 128:(c + 1) * 128], bt_[:, (c4 + c) * 64:(c4 + c + 1) * 64], AR[:, c4 + c, :, :].rearrange("p a t -> p (a t)"), start=True, stop=True),
                              reads=[T_], writes=[Tp])
                    fw.op("dve", lambda e: e.tensor_tensor(BM[:, c4:c4 + 4, :, :].rearrange("p c a t -> p (c a t)"), pp[:], maskSR[:], ALU.mult), reads=[Tp, cst], writes=[T_])
                    pp, Tp = ps4.next()
                    for c in range(4):
                        fw.op("pe", lambda e: e.matmul(pp[:, c * 64:(c + 1) * 64], AR[:, c4 + c, 0, :], bt_[:, (c4 + c) * 64:(c4 + c + 1) * 64], start=True, stop=True), reads=[T_], writes=[Tp])
                    fw.op("dve", lambda e: e.tensor_tensor(NT_[:, c4:c4 + 4, :].rearrange("p c t -> p (c t)"), pp[:, 0:256], maskSL[:, 0:256], ALU.mult), reads=[Tp, cst], writes=[T_])
            scA.__exit__(None, None, None)
            if delta:
                scB = fw.scope(); scB.__enter__()
                ps4 = Ring(fw, [64, 512], F32, 4, psum=True)
                Mk = fw.sb([64, NCH, 64]); Nk = NT_
                id64 = ident[0:64, 0:64]
                fw.op("dve", lambda e: e.tensor_copy(Mk[:], BM[:, :, 0, :]), reads=[T_], writes=[T_])
                for c in range(NCH):
                    fw.op("dve", lambda e: e.tensor_tensor(Tt[:, c, :], Mk[:, c, :], id64, ALU.add), reads=[T_, cst], writes=[T_])
                Mn = fw.sb([64, NCH, 64]); Nn = fw.sb([64, NCH, 64])
                for lev in range(5):
                    for c8 in range(0, NCH, 8):
                        pN, TpN = ps4.next(); pR, TpR = ps4.next()
                        if lev < 4:
                            pM, TpM = ps4.next()
                        for c in range(8):
                            cc = c8 + c
                            fw.op("pe", lambda e: e.matmul(pN[:, c * 64:(c + 1) * 64], Mk[:, cc, :], Nk[:, cc, :], start=True, stop=True), reads=[T_], writes=[TpN])
                            if lev < 4:
                                fw.op("pe", lambda e: e.matmul(pM[:, c * 64:(c + 1) * 64], Nk[:, cc, :], Mk[:, cc, :], start=True, stop=True), reads=[T_], writes=[TpM])
                        fw.op("act", lambda e: e.copy(Nn[:, c8:c8 + 8, :].rearrange("p c t -> p (c t)"), pN[:]), reads=[TpN], writes=[T_])
                        if lev < 4:
                            fw.op("dve", lambda e: e.tensor_copy(Mn[:, c8:c8 + 8, :].rearrange("p c t -> p (c t)"), pM[:]), reads=[TpM], writes=[T_])
                        for c in range(8):
                            cc = c8 + c
                            fw.op("pe", lambda e: e.matmul(pR[:, c * 64:(c + 1) * 64], Nn[:, cc, :], Tt[:, cc, :], start=True, stop=False), reads=[T_], writes=[TpR])
                            fw.op("pe", lambda e: e.matmul(pR[:, c * 64:(c + 1) * 64], id64, Tt[:, cc, :], start=False, stop=True), reads=[T_, cst], writes=[TpR])
                        fw.op("act", lambda e: e.copy(Tn[:, c8:c8 + 8, :].rearrange("p c t -> p (c t)"), pR[:]), reads=[TpR], writes=[T_])
                    Mk, Mn = Mn, Mk
                    Nk, Nn = Nn, Nk
                    Tt, Tn = Tn, Tt
                U["Tt"] = Tt
                scB.__exit__(None, None, None)
            return U

        def scan_pass(U, delta, V, TV, ST, TS, NI, Yout=None, TY=None):
            T_ = U["T"]; AR = U["AR"]; KM = U["KM"]; Kt = U["Kt"]; eL = U["eL"]
            psq = U["psq"]; wsb = U["wsb"]
            for c in range(NCH):
                if delta:
                    pw, Tpw = psq.next()
                    fw.op("pe", lambda e: e.matmul(pw[:, 0:NI], AR[:, c, 0, :], ST[:, 0:NI], start=True, stop=False), reads=[T_, TS], writes=[Tpw])
                    fw.op("pe", lambda e: e.matmul(pw[:, 0:NI], KM[:, c, 0, :], V[:, c, 0:NI], start=False, stop=True), reads=[T_, TV], writes=[Tpw])
                    w_, Tw_ = wsb.next()
                    fw.op("act", lambda e: e.copy(w_[:, 0:NI], pw[:, 0:NI]), reads=[Tpw], writes=[Tw_])
                    pu, Tpu = psq.next()
                    fw.op("pe", lambda e: e.matmul(pu[:, 0:NI], U["Tt"][:, c, :], w_[:, 0:NI], start=True, stop=True), reads=[T_, Tw_], writes=[Tpu])
                    u_, Tu_ = wsb.next()
                    fw.op("dve", lambda e: e.tensor_copy(u_[:, 0:NI], pu[:, 0:NI]), reads=[Tpu], writes=[Tu_])
                if Yout is not None:
                    py, Tpy = psq.next()
                    fw.op("pe", lambda e: e.matmul(py[:, 0:64], ST[:, 0:64], AR[:, c, 1, :], start=True, stop=False), reads=[T_, TS], writes=[Tpy])
                    if delta:
                        fw.op("pe", lambda e: e.matmul(py[:, 0:64], u_[:, 0:64], U["BM"][:, c, 1, :], start=False, stop=False), reads=[T_, Tu_], writes=[Tpy])
                    fw.op("pe", lambda e: e.matmul(py[:, 0:64], V[:, c, 0:64], KM[:, c, 1, :], start=False, stop=True), reads=[T_, TV], writes=[Tpy])
                    fw.op("act", lambda e: e.copy(Yout[:, c * 64:(c + 1) * 64], py[:, 0:64]), reads=[Tpy], writes=[TY])
                pn, Tpn = psq.next()
                if delta:
                    fw.op("pe", lambda e: e.matmul(pn[:, 0:NI], U["Bt"][:, c, :], u_[:, 0:NI], start=True, stop=False), reads=[T_, Tu_], writes=[Tpn])
                fw.op("pe", lambda e: e.matmul(pn[:, 0:NI], Kt[:, c, :], V[:, c, 0:NI], start=(not delta), stop=True), reads=[T_, TV], writes=[Tpn])
                fw.op("dve", lambda e: e.tensor_tensor(ST[:, 0:NI], ST[:, 0:NI], pn[:, 0:NI], ALU.add), reads=[Tpn], writes=[TS])
                fw.op("act", lambda e: e.activation(ST[:, 0:NI], ST[:, 0:NI], AF.Copy, scale=eL[:, c * 64 + 63:c * 64 + 64]), reads=[T_], writes=[TS])

        def mixers_scan(l):
            scan_pre(l)
            fw.barrier()
            lnw = din("rwkv_lnx_wH", [L, 64, 8]); lnb = din("rwkv_lnx_bH", [L, 64, 8]); gnT = din("gla_gnormT", [L, 128, 1])
            units = []
            for hd in range(8):
                rs = slice(hd * 64, (hd + 1) * 64)
                units.append((True, dict(r=s_rr[rs], k=s_rk[rs], ld=s_rld[rs], a=s_ra[rs], b=s_rb[rs]), 1.0, [s_rv[:, rs]], "rw", hd))
            for hd in range(4):
                rs = slice(hd * 64, (hd + 1) * 64)
                units.append((False, dict(r=s_glaq[rs], k=s_glak[rs], ld=s_gld[rs]), 0.125,
                              [s_glav[:, hd * 128:hd * 128 + 64], s_glav[:, hd * 128 + 64:(hd + 1) * 128]], "gla", hd))
            for (delta, srcs, rscale, vsrcs, kind, hd) in units:
                with fw.scope():
                    U = scan_unit_pre(delta, srcs, rscale)
                    U["psq"] = Ring(fw, [64, 128], F32, 3, psum=True)
                    U["wsb"] = Ring(fw, [64, 128], F32, 4)
                    for vi, vsrc in enumerate(vsrcs):
                        ui = hd if kind == "rw" else 8 + hd * 2 + vi
                        V = fw.sb([64, NCH, 128]); TV = Tk()
                        fw.op("dve", lambda e: e.memset(V[:, :, 64:128], 0.0), writes=[TV])
                        fw.dma("sp", V[:, :, 0:64], vsrc.rearrange("(c s) i -> s c i", s=64), TV, reads=[Tscr], writes=[TV])
                        ST = fw.sb([64, 128]); TS = Tk()
                        Sin = fw.sb([64, 64]); TSin = Tk()
                        fw.op("dve", lambda e: e.memset(Sin[:], 0.0), writes=[TSin])
                        if W > 1:
                            fw.op("dve", lambda e: e.tensor_copy(ST[:], s0aug[:]), reads=[cst], writes=[TS])
                            scan_pass(U, delta, V, TV, ST, TS, 128)
                            sst = nc.dram_tensor(f"s_st_{l}_{ui}", [64, 128], F32, kind="Internal").ap()
                            gst = nc.dram_tensor(f"g_st_{l}_{ui}", [W * 64, 128], F32, kind="Internal").ap()
                            fw.dma("pool", sst, ST[:], TS, reads=[TS], writes=[Tscr])
                            fw.barrier()
                            allgather(sst, gst)
                            fw.barrier()
                            G = fw.sb([64, W, 128]); TG = Tk()
                            fw.dma("sp", G[:], gst.rearrange("(r j) c -> j r c", j=64), TG, reads=[Tg], writes=[TG])
                            cur = fw.sb([64, 64]); Tcur = Tk(); xr = fw.sb([64, 64]); Txr = Tk()
                            fw.op("dve", lambda e: e.memset(cur[:], 0.0), writes=[Tcur])
                            for r_ in range(W):
                                fw.op("dve", lambda e: e.scalar_tensor_tensor(Sin[:], cur[:], selv[:, r_:r_ + 1], Sin[:], ALU.mult, ALU.add), reads=[Tcur, cst], writes=[TSin])
                                if r_ < W - 1:
                                    px, Tpx = U["psq"].next()
                                    fw.op("pe", lambda e: e.transpose(px[:, 0:64], G[:, r_, 64:128], ident[0:64, 0:64]), reads=[TG, cst], writes=[Tpx])
                                    fw.op("act", lambda e: e.copy(xr[:], px[:, 0:64]), reads=[Tpx], writes=[Txr])
                                    pc, Tpc = U["psq"].next()
                                    fw.op("pe", lambda e: e.matmul(pc[:, 0:64], xr[:], cur[:], start=True, stop=True), reads=[Txr, Tcur], writes=[Tpc])
                                    fw.op("dve", lambda e: e.tensor_tensor(cur[:], pc[:, 0:64], G[:, r_, 0:64], ALU.add), reads=[Tpc, TG], writes=[Tcur])
                        fw.op("dve", lambda e: e.tensor_copy(ST[:, 0:64], Sin[:]), reads=[TSin], writes=[TS])
                        Y = fw.sb([64, S]); TY = Tk()
                        scan_pass(U, delta, V, TV, ST, TS, 64, Y, TY)
                        if kind == "gla":
                            fw.dma("pool", s_graw[ui_row(ui):ui_row(ui) + 64, :], Y[:], TY, reads=[TY], writes=[Tscr])
                        else:
                            rw_post(l, hd, Y, TY, lnw, lnb)
            fw.barrier()
            gla_post(l, gnT)

        def ui_row(ui):
            return (ui - 8) * 64

        def rw_post(l, hd, Y, TY, lnw, lnb):
            lw = fw.sb([64, 2]); Tl = Tk()
            fw.dma("sp", lw[:, 0:1], lnw[l, :, hd:hd + 1], Tl, writes=[Tl], allow_slow_non_contiguous=True)
            fw.dma("sp", lw[:, 1:2], lnb[l, :, hd:hd + 1], Tl, writes=[Tl], allow_slow_non_contiguous=True)
            epsl = fw.sb([64, 1])
            fw.op("dve", lambda e: e.memset(epsl[:], 64e-5), writes=[Tl])
            pq = Ring(fw, [64, 512], F32, 2, psum=True)
            tq = Ring(fw, [64, 512], F32, 4)
            for t4 in range(S // 512):
                ts_ = slice(t4 * 512, (t4 + 1) * 512)
                pm, Tpm = pq.next()
                fw.op("pe", lambda e: e.matmul(pm[:], ones[0:64, 0:64], Y[:, ts_], start=True, stop=True), reads=[TY, cst], writes=[Tpm])
                d_, Td_ = tq.next()
                fw.op("dve", lambda e: e.scalar_tensor_tensor(d_[:], pm[:], -1.0 / 64, Y[:, ts_], ALU.mult, ALU.add), reads=[Tpm, TY], writes=[Td_])
                q_, Tq_ = tq.next()
                fw.op("act", lambda e: e.activation(q_[:], d_[:], AF.Square), reads=[Td_], writes=[Tq_])
                pv, Tpv = pq.next()
                fw.op("pe", lambda e: e.matmul(pv[:], ones[0:64, 0:64], q_[:], start=True, stop=True), reads=[Tq_, cst], writes=[Tpv])
                fw.op("act", lambda e: e.activation(q_[:], pv[:], AF.Sqrt, bias=epsl[:], scale=1.0 / 64), reads=[Tpv, Tl], writes=[Tq_])
                fw.op("dve", lambda e: e.reciprocal(q_[:], q_[:]), writes=[Tq_])
                fw.op("dve", lambda e: e.tensor_tensor(d_[:], d_[:], q_[:], ALU.mult), reads=[Tq_], writes=[Td_])
                fw.op("act", lambda e: e.activation(d_[:], d_[:], AF.Identity, bias=lw[:, 1:2], scale=lw[:, 0:1]), reads=[Tl], writes=[Td_])
                b_, Tb_ = tq.next()
                fw.dma("sp", b_[:], s_rbv[hd * 64:(hd + 1) * 64, ts_], Tb_, reads=[Tscr], writes=[Tb_])
                fw.op("dve", lambda e: e.tensor_tensor(d_[:], d_[:], b_[:], ALU.add), reads=[Tb_], writes=[Td_])
                g_, Tg_ = tq.next()
                fw.dma("sp", g_[:], s_rg[hd * 64:(hd + 1) * 64, ts_], Tg_, reads=[Tscr], writes=[Tg_])
                fw.op("dve", lambda e: e.tensor_tensor(d_[:], d_[:], g_[:], ALU.mult), reads=[Tg_], writes=[Td_])
                fw.dma("pool", s_o[768 + hd * 64:768 + (hd + 1) * 64, ts_], d_[:], Td_, reads=[Td_], writes=[Tscr])

        def gla_post(l, gnT):
            with fw.scope():
                gn = fw.sb([128, 1]); Tgn = Tk()
                fw.dma("sp", gn[:], gnT[l], Tgn, writes=[Tgn])
                pq = Ring(fw, [128, 512], F32, 2, psum=True)
                tq = Ring(fw, [128, 512], F32, 6)
                for hd in range(4):
                    for t4 in range(S // 512):
                        ts_ = slice(t4 * 512, (t4 + 1) * 512)
                        o_, To_ = tq.next()
                        fw.dma("sp", o_[:], s_graw[hd * 128:(hd + 1) * 128, ts_], To_, reads=[Tscr], writes=[To_])
                        q_, Tq_ = tq.next()
                        fw.op("act", lambda e: e.activation(q_[:], o_[:], AF.Square), reads=[To_], writes=[Tq_])
                        pm, Tpm = pq.next()
                        fw.op("pe", lambda e: e.matmul(pm[:], ones[:], q_[:], start=True, stop=True), reads=[Tq_, cst], writes=[Tpm])
                        fw.op("act", lambda e: e.activation(q_[:], pm[:], AF.Sqrt, bias=epsT[:], scale=1.0 / 128), reads=[Tpm, cst], writes=[Tq_])
                        fw.op("dve", lambda e: e.reciprocal(q_[:], q_[:]), writes=[Tq_])
                        fw.op("dve", lambda e: e.scalar_tensor_tensor(o_[:], o_[:], gn[:, 0:1], q_[:], ALU.mult, ALU.mult), reads=[Tq_, Tgn], writes=[To_])
                        g_, Tg_ = tq.next()
                        fw.dma("sp", g_[:], s_glag[hd * 128:(hd + 1) * 128, ts_], Tg_, reads=[Tscr], writes=[Tg_])
                        fw.op("act", lambda e: e.activation(g_[:], g_[:], AF.Silu), writes=[Tg_])
                        fw.op("dve", lambda e: e.tensor_tensor(o_[:], o_[:], g_[:], ALU.mult), reads=[Tg_], writes=[To_])
                        fw.dma("pool", s_o[hd * 128:(hd + 1) * 128, ts_], o_[:], To_, reads=[To_], writes=[Tscr])

        for l in range(L):
            src = xin if l == 0 else xT
            if not (dbg and "skipP" in dbg):
              with fw.scope():
                dn = Dense()
                cosT = fw.sb([32, 512]); sinT = fw.sb([32, 512]); Trope = Tk()
                rtmp = Ring(fw, [32, 512], F32, 3)
                posi = fw.sb([32, 512], I32); Tpos = Tk()
                ki = fw.sb([32, 512], I32)
                psr = Ring(fw, [32, 512], F32, 1, psum=True)
                vstg = Ring(fw, [128, 2, 129], F32, 2)
                for st_, Ts_ in vstg.b:
                    fw.op("dve", lambda e: e.memset(st_[:], 1.0), writes=[Ts_])
                for tt in range(NT):
                    xt, Tx = dn.load_x(src, tt)
                    ht, Th = dn.norm(xt, Tx, g1[l], lambda kc: modT[l][:, kc:kc + 1])
                    fw.dma("sp", posi[:], pos[0:1, tt * 512:(tt + 1) * 512].broadcast_to([32, 512]), Tpos, writes=[Tpos])
                    ang, Ta = rtmp.next()
                    fw.op("dve", lambda e: e.tensor_copy(ang[:], posi[:]), reads=[Tpos], writes=[Ta])
                    fw.op("dve", lambda e: e.tensor_scalar(ang[:], ang[:], invf[:, 0:1], None, ALU.mult), reads=[cst], writes=[Ta])
                    for which, dstT in ((0, sinT), (1, cosT)):
                        a2, Ta2 = rtmp.next()
                        fw.op("dve", lambda e: e.tensor_scalar(a2[:], ang[:], (math.pi / 2 if which else 0.0), None, ALU.add), reads=[Ta], writes=[Ta2])
                        fw.op("dve", lambda e: e.tensor_scalar(ki[:], a2[:], 1.0 / TWO_PI, None, ALU.mult), reads=[Ta2], writes=[Tpos])
                        kf, Tkf = rtmp.next()
                        fw.op("dve", lambda e: e.tensor_copy(kf[:], ki[:]), reads=[Tpos], writes=[Tkf])
                        for cc in (6.28125, 0.0019353071795864769, 0.0):
                            if cc == 0.0:
                                continue
                            fw.op("dve", lambda e: e.scalar_tensor_tensor(a2[:], kf[:], -cc, a2[:], ALU.mult, ALU.add), reads=[Tkf], writes=[Ta2])
                        fw.op("act", lambda e: e.activation(dstT[:], a2[:], AF.Sin), reads=[Ta2], writes=[Trope])
                    fw.op("dve", lambda e: e.tensor_scalar(sinT[:], sinT[:], sgn[:, 0:1], None, ALU.mult), reads=[cst], writes=[Trope])

                    def rope_post(scale):
                        def post(st, Ts, coff, mw):
                            pr, Tpr = psr.next()
                            fw.op("pe", lambda e: e.matmul(pr[:], permT[:], st[0:32, :], start=True, stop=True), reads=[Ts, cst], writes=[Tpr])
                            t1, Tt1 = rtmp.next()
                            fw.op("dve", lambda e: e.tensor_tensor(t1[:], pr[:], sinT[:], ALU.mult), reads=[Tpr, Trope], writes=[Tt1])
                            fw.op("dve", lambda e: e.tensor_tensor(st[0:32, :], st[0:32, :], cosT[:], ALU.mult), reads=[Trope], writes=[Ts])
                            fw.op("dve", lambda e: e.tensor_tensor(st[0:32, :], st[0:32, :], t1[:], ALU.add), reads=[Tt1], writes=[Ts])
                            if scale != 1.0:
                                fw.op("act", lambda e: e.mul(st[:], st[:], scale), writes=[Ts])
                        return post

                    def kmean_post(st, Ts, coff, mw):
                        rope_post(1.0)(st, Ts, coff, mw)
                        hd = coff // 128
                        fw.op("dve", lambda e: e.tensor_reduce(kmeanT[:, hd, tt * 2:tt * 2 + 2], st[:].rearrange("p (b t) -> p b t", t=256), AX.X, ALU.add),
                              reads=[Ts], writes=[Tkm])

                    def fm_group(col0, ncols, dst, post=None):
                        for c0 in range(0, ncols, 256):
                            cw = min(256, ncols - c0)
                            wt, Tw = dn.load_w(w_in[l], col0 + c0, cw)
                            for m0 in range(0, cw, 128):
                                mw = min(128, cw - m0)
                                pp, Tp = dn.psring.next()
                                for kc in range(NKC):
                                    fw.op("pe", lambda e: e.matmul(pp[0:mw, :], wt[:, kc, m0:m0 + mw], ht[:, kc, :], start=(kc == 0), stop=(kc == NKC - 1)),
                                          reads=[Tw, Th], writes=[Tp])
                                st, Ts = dn.stg.next()
                                fw.op("act", lambda e: e.copy(st[0:mw, :], pp[0:mw, :]), reads=[Tp], writes=[Ts])
                                if post is not None:
                                    post(st, Ts, c0 + m0, mw)
                                fw.dma("pool", dst[c0 + m0:c0 + m0 + mw, tt * 512:(tt + 1) * 512], st[0:mw, :], Ts, reads=[Ts], writes=[Tscr])

                    def tm_group_aug(col0, ncols, dst):
                        for c0 in range(0, ncols, 256):
                            wt, Tw = dn.load_w(w_in[l], col0 + c0, 256)
                            for ts in range(4):
                                pp, Tp = dn.psring.next()
                                for kc in range(NKC):
                                    fw.op("pe", lambda e: e.matmul(pp[:, 0:256], ht[:, kc, ts * 128:(ts + 1) * 128], wt[:, kc, 0:256], start=(kc == 0), stop=(kc == NKC - 1)),
                                          reads=[Tw, Th], writes=[Tp])
                                st, Ts = vstg.next()
                                fw.op("act", lambda e: e.copy(st[:, :, 0:128], pp[:, 0:256].rearrange("p (h e) -> p h e", e=128)), reads=[Tp], writes=[Ts])
                                r0 = tt * 512 + ts * 128
                                h0 = c0 // 128
                                fw.dma("pool", dst[r0:r0 + 128, h0 * 129:(h0 + 2) * 129], st[:].rearrange("p h e -> p (h e)"), Ts, reads=[Ts], writes=[Tscr])

                    def tm_group(col0, ncols, dst):
                        for c0 in range(0, ncols, 256):
                            cw = min(256, ncols - c0)
                            wt, Tw = dn.load_w(w_in[l], col0 + c0, cw)
                            for ts in range(4):
                                pp, Tp = dn.psring.next()
                                for kc in range(NKC):
                                    fw.op("pe", lambda e: e.matmul(pp[:, 0:cw], ht[:, kc, ts * 128:(ts + 1) * 128], wt[:, kc, 0:cw], start=(kc == 0), stop=(kc == NKC - 1)),
                                          reads=[Tw, Th], writes=[Tp])
                                st, Ts = dn.stg.next()
                                fw.op("act", lambda e: e.copy(st[:, 0:cw], pp[:, 0:cw]), reads=[Tp], writes=[Ts])
                                r0 = tt * 512 + ts * 128
                                fw.dma("pool", dst[r0:r0 + 128, c0:c0 + cw], st[:, 0:cw], Ts, reads=[Ts], writes=[Tscr])

                    qs = 128.0 ** -0.5
                    fm_group(O_GLA_Q, 256, s_glaq)
                    fm_group(O_GLA_K, 256, s_glak)
                    tm_group(O_GLA_V, 512, s_glav)
                    fm_group(O_GLA_G, 512, s_glag)
                    fm_group(O_GLA_A, 16, s_glaa)
                    fm_group(O_DIL_Q, 768, s_dilq, rope_post(qs))
                    fm_group(O_DIL_K, 768, s_dilk, rope_post(1.0))
                    tm_group_aug(O_DIL_V, 768, s_dilv)
                    fm_group(O_RW, 1984, s_rw)
                    fm_group(O_MO_Q, 512, s_moq, rope_post(qs))
                    fm_group(O_MO_K, 512, s_mok, kmean_post)
                    tm_group_aug(O_MO_V, 512, s_mov)
            if stop_after == "P":
                break

            if not (dbg and "skipMix" in dbg):
                if not (dbg and "noAttn" in dbg):
                    mixers_attn(l)
                if not (dbg and "noScan" in dbg):
                    mixers_scan(l)
            if stop_after == "mix":
                break

            with fw.scope():
                dn = Dense()
                big = fw.sb([128, 30, 512]); Tmg = Tk(); To = Tk()
                wbr = Ring(fw, [128, 4, 256], F32, 2)
                last = (l == L - 1)
                for tt in range(NT):
                    xt, Tx = dn.load_x(src, tt)
                    ht, Th = dn.norm(xt, Tx, g1[l], lambda kc: modT[l][:, kc:kc + 1])
                    fw.dma("sp", big[:, 16:30, :], s_o[:, tt * 512:(tt + 1) * 512].rearrange("(kc p) t -> p kc t", p=128), To, reads=[Tscr], writes=[To])
                    for dp in range(8):
                        for b in range(4):
                            wt, Tw = dn.load_w(w_in[l], b * 2048 + dp * 256, 256)
                            wb, Twb = wbr.next()
                            fw.dma("sp", wb[:, 0:BR_KC[b], :], w_br[b][l, :, dp * 256:(dp + 1) * 256].rearrange("(kc p) c -> p kc c", p=128), Twb, reads=[Twg], writes=[Twb])
                            for j in range(2):
                                dt_ = dp * 2 + j
                                pg, Tpg = dn.psring.next()
                                for kc in range(NKC):
                                    fw.op("pe", lambda e: e.matmul(pg[:], wt[:, kc, j * 128:(j + 1) * 128], ht[:, kc, :], start=(kc == 0), stop=(kc == NKC - 1)),
                                          reads=[Tw, Th], writes=[Tpg])
                                sg, Tsg = dn.stg.next()
                                fw.op("act", lambda e: e.activation(sg[:], pg[:], AF.Sigmoid), reads=[Tpg], writes=[Tsg])
                                pb, Tpb = dn.psring.next()
                                for kc in range(BR_KC[b]):
                                    fw.op("pe", lambda e: e.matmul(pb[:], wb[:, kc, j * 128:(j + 1) * 128], big[:, 16 + BR_ROW0[b] // 128 + kc, :],
                                                                   start=(kc == 0), stop=(kc == BR_KC[b] - 1)), reads=[Twb, To], writes=[Tpb])
                                if b == 0:
                                    fw.op("dve", lambda e: e.tensor_tensor(big[:, dt_, :], sg[:], pb[:], ALU.mult), reads=[Tsg, Tpb], writes=[Tmg])
                                else:
                                    fw.op("dve", lambda e: e.tensor_tensor(sg[:], sg[:], pb[:], ALU.mult), reads=[Tpb], writes=[Tsg])
                                    fw.op("dve", lambda e: e.tensor_tensor(big[:, dt_, :], big[:, dt_, :], sg[:], ALU.add), reads=[Tsg], writes=[Tmg])
                    for dp in range(8):
                        wt, Tw = dn.load_w(w_out[l], dp * 256, 256)
                        for j in range(2):
                            dt_ = dp * 2 + j
                            pp, Tp = dn.psring.next()
                            for kc in range(NKC):
                                fw.op("pe", lambda e: e.matmul(pp[:], wt[:, kc, j * 128:(j + 1) * 128], big[:, kc, :], start=(kc == 0), stop=(kc == NKC - 1)),
                                      reads=[Tw, Tmg], writes=[Tp])
                            fw.op("dve", lambda e: e.scalar_tensor_tensor(xt[:, dt_, :], pp[:], modT[l][:, 32 + dt_:33 + dt_], xt[:, dt_, :], ALU.mult, ALU.add),
                                  reads=[Tp, Tmod], writes=[Tx])
                    h2, Th2 = dn.norm(xt, Tx, g2[l], lambda kc: modT[l][:, 48 + kc:49 + kc])
                    for half in range(2):
                        for jj in range(22):
                            j = half * 22 + jj
                            buf = dn.wring.next()
                            dn.load_w(w_f1[l], j * 128, 128, dstcol=0, buf=buf)
                            wt, Tw = dn.load_w(w_f1[l], FFN_H + j * 128, 128, dstcol=128, buf=buf)
                            pg, Tpg = dn.psring.next(); pu, Tpu = dn.psring.next()
                            for kc in range(NKC):
                                fw.op("pe", lambda e: e.matmul(pg[:], wt[:, kc, 0:128], h2[:, kc, :], start=(kc == 0), stop=(kc == NKC - 1)), reads=[Tw, Th2], writes=[Tpg])
                            for kc in range(NKC):
                                fw.op("pe", lambda e: e.matmul(pu[:], wt[:, kc, 128:256], h2[:, kc, :], start=(kc == 0), stop=(kc == NKC - 1)), reads=[Tw, Th2], writes=[Tpu])
                            sg, Tsg = dn.stg.next()
                            fw.op("act", lambda e: e.activation(sg[:], pg[:], AF.Silu), reads=[Tpg], writes=[Tsg])
                            fw.op("dve", lambda e: e.tensor_tensor(big[:, jj, :], sg[:], pu[:], ALU.mult), reads=[Tsg, Tpu], writes=[Tmg, To])
                        for dp in range(8):
                            pps = [dn.psring.next(), dn.psring.next()]
                            for kg in range(2):
                                wt, Tw = dn.load_w(w_f2[l], dp * 256, 256, k0=half * 22 + kg * 11, nk=11)
                                for j in range(2):
                                    pp, Tp = pps[j]
                                    for kc in range(11):
                                        fw.op("pe", lambda e: e.matmul(pp[:], wt[:, kc, j * 128:(j + 1) * 128], big[:, kg * 11 + kc, :],
                                                                       start=(kg == 0 and kc == 0), stop=(kg == 1 and kc == 10)), reads=[Tw, Tmg, To], writes=[Tp])
                            for j in range(2):
                                dt_ = dp * 2 + j
                                pp, Tp = pps[j]
                                fw.op("dve", lambda e: e.scalar_tensor_tensor(xt[:, dt_, :], pp[:], modT[l][:, 80 + dt_:81 + dt_], xt[:, dt_, :], ALU.mult, ALU.add),
                                      reads=[Tp, Tmod], writes=[Tx])
                    if not last:
                        fw.dma("pool", xT[:, tt * 512:(tt + 1) * 512].rearrange("(kc p) t -> p kc t", p=128), xt[:], Tx, reads=[Tx], writes=[Txd])
                    else:
                        hf, Thf = dn.norm(xt, Tx, nfT, None)
                        fw.dma("pool", outT[:, tt * 512:(tt + 1) * 512].rearrange("(kc p) t -> p kc t", p=128), hf[:], Thf, reads=[Thf], writes=[Txd])

        fw.barrier()
        print("ninst", fw.ninst)
    return nc, ins


def host_consts():
    c = {}
    c["c_ones"] = np.ones((128, 128), np.float32)
    c["c_ident"] = np.eye(128, dtype=np.float32)
    inv = (500000.0 ** (-np.arange(0, 32, 2, dtype=np.float32) / 32)).astype(np.float32)
    c["c_invf"] = np.concatenate([inv, inv]).reshape(32, 1).astype(np.float32)
    c["c_sgn"] = np.concatenate([-np.ones(16), np.ones(16)]).reshape(32, 1).astype(np.float32)
    P = np.zeros((32, 32), np.float32)
    for i in range(16):
        P[i + 16, i] = 1.0
        P[i, i + 16] = 1.0
    c["c_permT"] = P
    k = np.arange(128)[:, None]; q = np.arange(128)[None, :]
    c["c_dmask"] = np.concatenate([(k >= q), (k <= q)], axis=1).astype(np.float32)
    es = np.zeros((64, 64, 128), np.float32)
    for n in range(64):
        es[n, n, :] = 1.0
    c["c_esel"] = es.reshape(64, 64 * 128)
    b = np.zeros((128, 128), np.float32); b[:64, :64] = 1; b[64:, 64:] = 1
    c["c_blk64"] = b
    si = np.arange(64)[:, None]; ti = np.arange(64)[None, :]
    su = (si < ti).astype(np.float32); iu = (si <= ti).astype(np.float32); sl = (si > ti).astype(np.float32)
    c["c_maskSR"] = np.tile(np.concatenate([su, iu], axis=1), (1, 4))
    c["c_maskSL"] = np.tile(sl, (1, 8))
    c["c_s0aug"] = np.concatenate([np.zeros((64, 64), np.float32), np.eye(64, dtype=np.float32)], axis=1)
    return c


def moba_masks(rank, S, W):
    nbl = S // 256
    cb = np.zeros((nbl, 128, 64), np.float32); vm = np.zeros((nbl, 128, 64), np.float32)
    for QB in range(nbl):
        Bq = rank * nbl + QB
        cb[QB, :, Bq:] = -1e30
        vm[QB, :, :Bq] = 30000.0
    return cb, vm


def prep_inputs(inp, S, L):
    m = dict(host_consts())
    m["xT"] = np.ascontiguousarray(inp["x"][0, :S].T)
    m["cT"] = np.ascontiguousarray(inp["c"].reshape(16, 128).T)
    m["pos"] = np.ascontiguousarray(inp["positions"][:, :S]).astype(np.int32)
    m["w_ada"] = np.ascontiguousarray(inp["w_ada"][:L])
    m["b_adaT"] = np.ascontiguousarray(inp["b_ada"][:L].reshape(L, 96, 128).transpose(0, 2, 1))
    m["norm1T"] = np.ascontiguousarray(inp["norm1"][:L].reshape(L, 16, 128).transpose(0, 2, 1))
    m["norm2T"] = np.ascontiguousarray(inp["norm2"][:L].reshape(L, 16, 128).transpose(0, 2, 1))
    m["normfT"] = np.ascontiguousarray(inp["norm_f"].reshape(16, 128).T)
    m["w_in"] = np.ascontiguousarray(inp["w_in"][:L])
    return m


def prep_weights(inp, L):
    m = {}
    for k in ("w_branch_a", "w_branch_b", "w_branch_c", "w_branch_d", "w_out", "w_ffn_in", "w_ffn_out"):
        m[k] = np.ascontiguousarray(inp[k][:L])
    return m


RW_GROUPS_H = [(i * 128, 128) for i in range(12)] + [(1536, 96), (1632, 96), (1728, 128), (1856, 128)]


def prep_mixer_params(inp, L, rank=0, W=1):
    m = {}
    mu = inp["rwkv_mu"][:L]
    muG = np.zeros((L, 128, 16), np.float32)
    for gi, (r0, n) in enumerate(RW_GROUPS_H):
        muG[:, :n, gi] = mu[:, r0:r0 + n]
    m["rwkv_muG"] = muG
    for nm in ("w0", "a0", "k_k", "k_a", "r_k"):
        m[f"rwkv_{nm}T"] = np.ascontiguousarray(inp[f"rwkv_{nm}"][:L].reshape(L, 4, 128).transpose(0, 2, 1))
    m["rwkv_w2"] = np.ascontiguousarray(inp["rwkv_w2"][:L]); m["rwkv_a2"] = np.ascontiguousarray(inp["rwkv_a2"][:L])
    m["rwkv_g2"] = np.ascontiguousarray(inp["rwkv_g2"][:L])
    if L > 1:
        m["rwkv_v0T"] = np.ascontiguousarray(inp["rwkv_v0"][:L - 1].reshape(L - 1, 4, 128).transpose(0, 2, 1))
        m["rwkv_v1"] = np.ascontiguousarray(inp["rwkv_v1"][:L - 1]); m["rwkv_v2"] = np.ascontiguousarray(inp["rwkv_v2"][:L - 1])
    m["rwkv_lnx_wH"] = np.ascontiguousarray(inp["rwkv_lnx_w"][:L].reshape(L, 8, 64).transpose(0, 2, 1))
    m["rwkv_lnx_bH"] = np.ascontiguousarray(inp["rwkv_lnx_b"][:L].reshape(L, 8, 64).transpose(0, 2, 1))
    m["gla_w_a2"] = np.ascontiguousarray(inp["gla_w_a2"][:L])
    m["gla_b_a2T"] = np.ascontiguousarray(inp["gla_b_a2"][:L].reshape(L, 2, 128).transpose(0, 2, 1))
    m["gla_gnormT"] = np.ascontiguousarray(inp["gla_gnorm"][:L].reshape(L, 128, 1))
    sel = np.zeros((64, W), np.float32); sel[:, rank] = 1.0
    m["c_selv"] = sel
    selp = np.zeros((128, W), np.float32)
    if rank >= 1:
        selp[:, rank - 1] = 1.0
    m["c_selp"] = selp
    return m


W_CORES = 8
S_LOCAL = 2048
_SHARD_W = ("w_in", "w_branch_a", "w_branch_b", "w_branch_c", "w_branch_d", "w_out", "w_ffn_in", "w_ffn_out")


def make_in_maps(inp, W, S, L, ins):
    common = dict(host_consts())
    common["cT"] = np.ascontiguousarray(inp["c"].reshape(16, 128).T)
    common["b_adaT"] = np.ascontiguousarray(inp["b_ada"][:L].reshape(L, 96, 128).transpose(0, 2, 1))
    common["norm1T"] = np.ascontiguousarray(inp["norm1"][:L].reshape(L, 16, 128).transpose(0, 2, 1))
    common["norm2T"] = np.ascontiguousarray(inp["norm2"][:L].reshape(L, 16, 128).transpose(0, 2, 1))
    common["normfT"] = np.ascontiguousarray(inp["norm_f"].reshape(16, 128).T)
    maps = []
    for r in range(W):
        m = dict(common)
        m.update(prep_mixer_params(inp, L, rank=r, W=W))
        m["xT"] = np.ascontiguousarray(inp["x"][0, r * S:(r + 1) * S].T)
        m["pos"] = np.ascontiguousarray(inp["positions"][:, r * S:(r + 1) * S]).astype(np.int32)
        cb, vm = moba_masks(r, S, W)
        m["mo_cb"] = cb; m["mo_vm"] = vm
        for k in _SHARD_W:
            w = inp[k][:L]
            if W == 1:
                m[k] = np.ascontiguousarray(w)
            else:
                RS = 128 // W
                Lw, K, N = w.shape
                if k == "w_ffn_in":
                    pe = 128 * 2048
                    npc = (K * N) // (pe * W)
                    m[k + "_sh"] = np.ascontiguousarray(w.reshape(Lw, npc, W, 128, 2048)[:, :, r].reshape(Lw, npc * 128, 2048))
                else:
                    m[k + "_sh"] = np.ascontiguousarray(w.reshape(Lw, K // 128, W, RS, N)[:, :, r].reshape(Lw, K // W, N))
        nct = 96 // W
        if W == 1:
            m["w_ada"] = np.ascontiguousarray(inp["w_ada"][:L])
        else:
            m["w_ada_cs"] = np.ascontiguousarray(inp["w_ada"][:L, :, r * nct * 128:(r + 1) * nct * 128])
            m["b_adaT"] = np.ascontiguousarray(common["b_adaT"][:, :, r * nct:(r + 1) * nct])
        maps.append({k: np.asarray(v) for k, v in m.items() if k in ins})
    return maps


def kernel(**inputs):
    inp = {k: np.asarray(v) for k, v in inputs.items()}
    W, S, L = W_CORES, S_LOCAL, 2
    nc, ins = build(S, L, W=W)
    maps = make_in_maps(inp, W, S, L, ins)
    missing = set(ins) - set(maps[0])
    assert not missing, missing
    res = run_bass_kernel_spmd(nc, maps, core_ids=list(range(W)))
    outs = [np.asarray(res.results[r]["outT"]).T for r in range(W)]
    out = np.concatenate(outs, axis=0)[None].astype(np.float32)
    return out
```
